# Optimizing a Trainium2 kernel written in Bass

```python
import math
import jax
import jax.numpy as jnp
from jax import lax
import numpy as np

D_MODEL = 1024
BATCH = 8
SEQ = 2048
DEPTH = 4

GRID_W = 64
CTX_LEN = 256
N_MIXERS = 4
N_MOD = 9
D_FF = 2816
ROPE_THETA = 10000.0
Q_BLOCK = 128
NEG_INF = -1e30
EPS = 1e-6

DA_HEAD_DIM = 64
DA_HEADS = D_MODEL // (2 * DA_HEAD_DIM)
GQA_HEAD_DIM = 64
GQA_HEADS = D_MODEL // GQA_HEAD_DIM
GQA_KV_HEADS = 4
GQA_GROUP = GQA_HEADS // GQA_KV_HEADS
MLA_HEADS = 16
MLA_Q_LORA = 256
MLA_KV_LORA = 128
MLA_NOPE = 64
MLA_ROPE = 32
MLA_V = 64
NA_HEAD_DIM = 64
NA_HEADS = D_MODEL // NA_HEAD_DIM
NA_WIN_ROWS = 8
NA_WIN_COLS = 16

kernel_name = "hybrid_diffusion_trunk_prefix_ctx"


def rms_norm(x, g):
    xf = x.astype(jnp.float32)
    y = xf * lax.rsqrt(jnp.mean(xf * xf, axis=-1, keepdims=True) + EPS)
    return (y * g.astype(jnp.float32)).astype(x.dtype)


def modulate(x, g, shift, scale):
    return rms_norm(x, g) * (1 + scale) + shift


def swiglu(h, w_in, w_out):
    gate, up = jnp.split(h @ w_in, 2, axis=-1)
    return (jax.nn.silu(gate) * up) @ w_out


def axial_rope(x, row, col):
    rd = x.shape[-1]
    half = rd // 2
    nf = half // 2
    inv_freq = ROPE_THETA ** (-jnp.arange(nf, dtype=jnp.float32) / nf)

    def rot(xh, pos):
        ang = pos.astype(jnp.float32)[:, None] * inv_freq
        shape = (ang.shape[0],) + (1,) * (x.ndim - 3) + (nf,)
        cos = jnp.cos(ang).reshape(shape).astype(x.dtype)
        sin = jnp.sin(ang).reshape(shape).astype(x.dtype)
        x1, x2 = jnp.split(xh, 2, axis=-1)
        return jnp.concatenate([x1 * cos - x2 * sin, x1 * sin + x2 * cos], axis=-1)

    return jnp.concatenate([rot(x[..., :half], row), rot(x[..., half:], col)], axis=-1)


def sweep_query_blocks(attend, q):
    b, l = q.shape[:2]
    nb = l // Q_BLOCK
    qb = jnp.moveaxis(q.reshape((b, nb, Q_BLOCK) + q.shape[2:]), 1, 0)
    out = lax.map(attend, qb)
    return jnp.moveaxis(out, 0, 1).reshape((b, l) + out.shape[3:])


def gqa_attend(q, k, v, scale):
    s = jnp.einsum("bqhgd,bkhd->bhgqk", q, k).astype(jnp.float32) * scale
    p = jax.nn.softmax(s, axis=-1).astype(v.dtype)
    return jnp.einsum("bhgqk,bkhd->bqhgd", p, v)


def _da_project(t, w_qkv):
    bt, lt, _ = t.shape
    q, k, v = jnp.split(t @ w_qkv, 3, axis=-1)
    q = q.reshape(bt, lt, DA_HEADS, 2, DA_HEAD_DIM)
    k = k.reshape(bt, lt, DA_HEADS, 2, DA_HEAD_DIM)
    v = v.reshape(bt, lt, DA_HEADS, 2 * DA_HEAD_DIM)
    return q, k, v


def _diff_attend(q, k, v, lam):
    s = jnp.einsum("bqhmd,bkhmd->bhmqk", q, k).astype(jnp.float32) * DA_HEAD_DIM ** -0.5
    p = jax.nn.softmax(s, axis=-1)
    p = (p[:, :, 0] - lam * p[:, :, 1]).astype(v.dtype)
    return jnp.einsum("bhqk,bkhe->bqhe", p, v)


def diff_attention(h, hc, w_qkv, lam_q1, lam_k1, lam_q2, lam_k2, subln_g, w_o, layer_idx, row, col, with_ctx):
    b, l, _ = h.shape
    lam_init = 0.8 - 0.6 * math.exp(-0.3 * layer_idx)
    lam = (jnp.exp(jnp.sum(lam_q1 * lam_k1).astype(jnp.float32))
           - jnp.exp(jnp.sum(lam_q2 * lam_k2).astype(jnp.float32)) + lam_init)
    q, k, v = _da_project(h, w_qkv)
    qc, kc, vc = _da_project(hc, w_qkv)
    q = axial_rope(q, row, col)
    k = axial_rope(k, row, col)
    k_all = jnp.concatenate([k, kc], axis=1)
    v_all = jnp.concatenate([v, vc], axis=1)

    def finish(o):
        o = rms_norm(o, subln_g) * (1 - lam_init)
        return o.reshape(o.shape[0], o.shape[1], -1) @ w_o

    y = finish(sweep_query_blocks(lambda qb: _diff_attend(qb, k_all, v_all, lam), q))
    yc = finish(_diff_attend(qc, kc, vc, lam)) if with_ctx else None
    return y, yc


def _gqa_project(t, w_qkv, q_norm_g, k_norm_g):
    bt, lt, _ = t.shape
    q, k, v = jnp.split(t @ w_qkv, [GQA_HEADS * GQA_HEAD_DIM, (GQA_HEADS + GQA_KV_HEADS) * GQA_HEAD_DIM], axis=-1)
    q = rms_norm(q.reshape(bt, lt, GQA_KV_HEADS, GQA_GROUP, GQA_HEAD_DIM), q_norm_g)
    k = rms_norm(k.reshape(bt, lt, GQA_KV_HEADS, GQA_HEAD_DIM), k_norm_g)
    v = v.reshape(bt, lt, GQA_KV_HEADS, GQA_HEAD_DIM)
    return q, k, v


def gqa_attention(h, hc, w_qkv, q_norm_g, k_norm_g, w_o, row, col, with_ctx):
    scale = GQA_HEAD_DIM ** -0.5
    q, k, v = _gqa_project(h, w_qkv, q_norm_g, k_norm_g)
    qc, kc, vc = _gqa_project(hc, w_qkv, q_norm_g, k_norm_g)
    q = axial_rope(q, row, col)
    k = axial_rope(k, row, col)
    k_all = jnp.concatenate([k, kc], axis=1)
    v_all = jnp.concatenate([v, vc], axis=1)
    o = sweep_query_blocks(lambda qb: gqa_attend(qb, k_all, v_all, scale), q)
    y = o.reshape(o.shape[0], o.shape[1], -1) @ w_o
    yc = None
    if with_ctx:
        oc = gqa_attend(qc, kc, vc, scale)
        yc = oc.reshape(oc.shape[0], oc.shape[1], -1) @ w_o
    return y, yc


def _mla_project(t, w_down, q_norm_g, kv_norm_g, w_uq, w_ukv, row, col):
    bt, lt, _ = t.shape
    cq, ckv, k_pe = jnp.split(t @ w_down, [MLA_Q_LORA, MLA_Q_LORA + MLA_KV_LORA], axis=-1)
    q = (rms_norm(cq, q_norm_g) @ w_uq).reshape(bt, lt, MLA_HEADS, MLA_NOPE + MLA_ROPE)
    kv = (rms_norm(ckv, kv_norm_g) @ w_ukv).reshape(bt, lt, MLA_HEADS, MLA_NOPE + MLA_V)
    q_nope, q_pe = jnp.split(q, [MLA_NOPE], axis=-1)
    k_nope, v = jnp.split(kv, [MLA_NOPE], axis=-1)
    k_pe = k_pe[:, :, None, :]
    if row is not None:
        q_pe = axial_rope(q_pe, row, col)
        k_pe = axial_rope(k_pe, row, col)
    q = jnp.concatenate([q_nope, q_pe], axis=-1)[:, :, :, None, :]
    k = jnp.concatenate([k_nope, jnp.broadcast_to(k_pe, (bt, lt, MLA_HEADS, MLA_ROPE))], axis=-1)
    return q, k, v


def mla_attention(h, hc, w_down, q_norm_g, kv_norm_g, w_uq, w_ukv, w_o, row, col, with_ctx):
    scale = (MLA_NOPE + MLA_ROPE) ** -0.5
    q, k, v = _mla_project(h, w_down, q_norm_g, kv_norm_g, w_uq, w_ukv, row, col)
    qc, kc, vc = _mla_project(hc, w_down, q_norm_g, kv_norm_g, w_uq, w_ukv, None, None)
    k_all = jnp.concatenate([k, kc], axis=1)
    v_all = jnp.concatenate([v, vc], axis=1)
    o = sweep_query_blocks(lambda qb: gqa_attend(qb, k_all, v_all, scale), q)
    y = o.reshape(o.shape[0], o.shape[1], -1) @ w_o
    yc = None
    if with_ctx:
        oc = gqa_attend(qc, kc, vc, scale)
        yc = oc.reshape(oc.shape[0], oc.shape[1], -1) @ w_o
    return y, yc


def _na_project(t, w_qkv):
    bt, lt, _ = t.shape
    q, k, v = jnp.split(t @ w_qkv, 3, axis=-1)
    shp = (bt, lt, NA_HEADS, NA_HEAD_DIM)
    return q.reshape(shp), k.reshape(shp), v.reshape(shp)


def neighbourhood_attention(h, hc, w_qkv, rpb, w_o, with_ctx):
    b, l, _ = h.shape
    rows = l // GRID_W
    wr = min(NA_WIN_ROWS, rows)
    scale = NA_HEAD_DIM ** -0.5
    q, k, v = _na_project(h, w_qkv)
    qc, kc, vc = _na_project(hc, w_qkv)
    q = q.reshape(b, rows, GRID_W, NA_HEADS, NA_HEAD_DIM)
    k = k.reshape(b, rows, GRID_W, NA_HEADS, NA_HEAD_DIM)
    v = v.reshape(b, rows, GRID_W, NA_HEADS, NA_HEAD_DIM)

    cols = jnp.arange(GRID_W)
    col_start = jnp.clip(cols - NA_WIN_COLS // 2, 0, GRID_W - NA_WIN_COLS)
    col_in = (cols[None, :] >= col_start[:, None]) & (cols[None, :] < col_start[:, None] + NA_WIN_COLS)
    col_idx = jnp.clip(cols[None, :] - cols[:, None] + NA_WIN_COLS - 1, 0, 2 * NA_WIN_COLS - 2)
    row_ids = jnp.arange(rows)
    row_start = jnp.clip(row_ids - wr // 2, 0, rows - wr)

    def attend_row(args):
        q_r, r, rs = args
        k_band = lax.dynamic_slice_in_dim(k, rs, wr, axis=1)
        v_band = lax.dynamic_slice_in_dim(v, rs, wr, axis=1)
        row_idx = rs + jnp.arange(wr) - r + (NA_WIN_ROWS - 1)
        bias = jnp.transpose(rpb[:, row_idx][:, :, col_idx], (0, 2, 1, 3)).astype(jnp.float32)
        s_nb = jnp.einsum("bqhd,bjkhd->bhqjk", q_r, k_band).astype(jnp.float32) * scale + bias
        s_nb = jnp.where(col_in[:, None, :], s_nb, NEG_INF).reshape(b, NA_HEADS, GRID_W, wr * GRID_W)
        s_cx = jnp.einsum("bqhd,bchd->bhqc", q_r, kc).astype(jnp.float32) * scale
        p = jax.nn.softmax(jnp.concatenate([s_nb, s_cx], axis=-1), axis=-1).astype(v.dtype)
        p_nb = p[..., :wr * GRID_W].reshape(b, NA_HEADS, GRID_W, wr, GRID_W)
        p_cx = p[..., wr * GRID_W:]
        return (jnp.einsum("bhqjk,bjkhd->bqhd", p_nb, v_band)
                + jnp.einsum("bhqc,bchd->bqhd", p_cx, vc))

    o = lax.map(attend_row, (jnp.moveaxis(q, 1, 0), row_ids, row_start))
    y = jnp.moveaxis(o, 0, 1).reshape(b, l, NA_HEADS * NA_HEAD_DIM) @ w_o
    yc = None
    if with_ctx:
        oc = gqa_attend(qc[:, :, :, None, :], kc, vc, scale)
        yc = oc.reshape(oc.shape[0], oc.shape[1], -1) @ w_o
    return y, yc


def setup_inputs(seed: int = 0) -> dict:
    key = jax.random.key(seed)
    keys = iter(jax.random.split(key, 32))
    f32 = jnp.float32
    D = D_MODEL

    def normal(shape, std):
        return jax.random.normal(next(keys), shape, f32) * std

    def gain(shape):
        return 1.0 + normal(shape, 0.05)

    n_a, n_b, n_c, n_d = [len(range(m, DEPTH, N_MIXERS)) for m in range(N_MIXERS)]
    da_width = DA_HEADS * 2 * DA_HEAD_DIM
    gqa_width = GQA_HEADS * GQA_HEAD_DIM
    na_width = NA_HEADS * NA_HEAD_DIM
    return {
        "x": normal((BATCH, SEQ, D), 1.0),
        "c": normal((BATCH, D), 1.0),
        "ctx": normal((BATCH, CTX_LEN, D), 1.0),
        "c_ctx": normal((D,), 1.0),
        "w_mod": normal((DEPTH, D, N_MOD * D), 0.5 * D ** -0.5),
        "b_mod": normal((DEPTH, N_MOD * D), 0.02),
        "norm_g": gain((DEPTH, 3, D)),
        "w_ffn_in": normal((DEPTH, 2, D, 2 * D_FF), D ** -0.5),
        "w_ffn_out": normal((DEPTH, 2, D_FF, D), D_FF ** -0.5),
        "da_w_qkv": normal((n_a, D, 3 * da_width), D ** -0.5),
        "da_lam_q1": normal((n_a, DA_HEAD_DIM), 0.1),
        "da_lam_k1": normal((n_a, DA_HEAD_DIM), 0.1),
        "da_lam_q2": normal((n_a, DA_HEAD_DIM), 0.1),
        "da_lam_k2": normal((n_a, DA_HEAD_DIM), 0.1),
        "da_subln_g": gain((n_a, 2 * DA_HEAD_DIM)),
        "da_w_o": normal((n_a, da_width, D), da_width ** -0.5),
        "gqa_w_qkv": normal((n_b, D, (GQA_HEADS + 2 * GQA_KV_HEADS) * GQA_HEAD_DIM), D ** -0.5),
        "gqa_q_norm_g": gain((n_b, GQA_HEAD_DIM)),
        "gqa_k_norm_g": gain((n_b, GQA_HEAD_DIM)),
        "gqa_w_o": normal((n_b, gqa_width, D), gqa_width ** -0.5),
        "mla_w_down": normal((n_c, D, MLA_Q_LORA + MLA_KV_LORA + MLA_ROPE), D ** -0.5),
        "mla_q_norm_g": gain((n_c, MLA_Q_LORA)),
        "mla_kv_norm_g": gain((n_c, MLA_KV_LORA)),
        "mla_w_uq": normal((n_c, MLA_Q_LORA, MLA_HEADS * (MLA_NOPE + MLA_ROPE)), MLA_Q_LORA ** -0.5),
        "mla_w_ukv": normal((n_c, MLA_KV_LORA, MLA_HEADS * (MLA_NOPE + MLA_V)), MLA_KV_LORA ** -0.5),
        "mla_w_o": normal((n_c, MLA_HEADS * MLA_V, D), (MLA_HEADS * MLA_V) ** -0.5),
        "na_w_qkv": normal((n_d, D, 3 * na_width), D ** -0.5),
        "na_rpb": normal((n_d, NA_HEADS, 2 * NA_WIN_ROWS - 1, 2 * NA_WIN_COLS - 1), 0.1),
        "na_w_o": normal((n_d, na_width, D), na_width ** -0.5),
        "final_g": gain((D,)),
    }


def reference(x, c, ctx, c_ctx, w_mod, b_mod, norm_g, w_ffn_in, w_ffn_out,
              da_w_qkv, da_lam_q1, da_lam_k1, da_lam_q2, da_lam_k2, da_subln_g, da_w_o,
              gqa_w_qkv, gqa_q_norm_g, gqa_k_norm_g, gqa_w_o,
              mla_w_down, mla_q_norm_g, mla_kv_norm_g, mla_w_uq, mla_w_ukv, mla_w_o,
              na_w_qkv, na_rpb, na_w_o, final_g):
    b, l, d = x.shape
    t = jnp.arange(l)
    row = t // GRID_W
    col = t % GRID_W
    silu_c = jax.nn.silu(c)
    silu_cc = jax.nn.silu(c_ctx)
    xc = ctx
    for i in range(DEPTH):
        kind, inst = i % N_MIXERS, i // N_MIXERS
        with_ctx = i < DEPTH - 1
        mod = (silu_c @ w_mod[i] + b_mod[i]).reshape(b, N_MOD, 1, d)
        mod_c = (silu_cc @ w_mod[i] + b_mod[i]).reshape(N_MOD, d)

        x = x + 0.5 * mod[:, 2] * swiglu(modulate(x, norm_g[i, 0], mod[:, 0], mod[:, 1]), w_ffn_in[i, 0], w_ffn_out[i, 0])
        xc = xc + 0.5 * mod_c[2] * swiglu(modulate(xc, norm_g[i, 0], mod_c[0], mod_c[1]), w_ffn_in[i, 0], w_ffn_out[i, 0])

        h = modulate(x, norm_g[i, 1], mod[:, 3], mod[:, 4])
        hc = modulate(xc, norm_g[i, 1], mod_c[3], mod_c[4])
        if kind == 0:
            y, yc = diff_attention(h, hc, da_w_qkv[inst], da_lam_q1[inst], da_lam_k1[inst], da_lam_q2[inst],
                                   da_lam_k2[inst], da_subln_g[inst], da_w_o[inst], i, row, col, with_ctx)
        elif kind == 1:
            y, yc = gqa_attention(h, hc, gqa_w_qkv[inst], gqa_q_norm_g[inst], gqa_k_norm_g[inst], gqa_w_o[inst],
                                  row, col, with_ctx)
        elif kind == 2:
            y, yc = mla_attention(h, hc, mla_w_down[inst], mla_q_norm_g[inst], mla_kv_norm_g[inst], mla_w_uq[inst],
                                  mla_w_ukv[inst], mla_w_o[inst], row, col, with_ctx)
        else:
            y, yc = neighbourhood_attention(h, hc, na_w_qkv[inst], na_rpb[inst], na_w_o[inst], with_ctx)
        x = x + mod[:, 5] * y

        x = x + 0.5 * mod[:, 8] * swiglu(modulate(x, norm_g[i, 2], mod[:, 6], mod[:, 7]), w_ffn_in[i, 1], w_ffn_out[i, 1])
        if with_ctx:
            xc = xc + mod_c[5] * yc
            xc = xc + 0.5 * mod_c[8] * swiglu(modulate(xc, norm_g[i, 2], mod_c[6], mod_c[7]), w_ffn_in[i, 1], w_ffn_out[i, 1])
    return rms_norm(x, final_g)
```

```python
import contextlib
import math
import numpy as np
import concourse.bass as bass
import concourse.mybir as mybir
from concourse.bass_utils import run_bass_kernel_spmd

F32 = mybir.dt.float32
BF16 = mybir.dt.bfloat16
AF = mybir.ActivationFunctionType
ALU = mybir.AluOpType

D = 1024
NCH = 8
SEQ = 2048
CTX = 256
NTOK = SEQ + CTX
DFF = 2816
NFF = 22
DEPTH = 4
EPS = 1e-6
GRID_W = 64
BLOCKS = [(0, 512, False), (512, 512, False), (1024, 512, False), (1536, 512, False), (2048, 256, True)]

V_C, V_CC, V_FG, V_NG, V_BM = 0, 8, 16, 24, 128
V_QN, V_KN, V_SUB, V_MQN, V_MKN = 416, 417, 418, 419, 421
NVEC = 512


class Res:
    __slots__ = ("name", "w", "r")

    def __init__(self, name):
        self.name = name
        self.w = None
        self.r = {}


class Tracker:
    def __init__(self):
        self.prog = {e: [] for e in ("pe", "act", "dve", "pool", "sp")}
        self.cnt = {}
        self.seen = {e: {} for e in self.prog}
        self.nsem_dma = 0

    def _waits(self, en, reads, writes):
        need = {}

        def req(ev):
            if ev is None:
                return
            k, c = ev
            if need.get(k, 0) < c:
                need[k] = c
        for r in reads:
            req(r.w)
        for w in writes:
            req(w.w)
            for k, c in w.r.items():
                req((k, c))
        for k, c in need.items():
            if k == en and en in ("pe",):
                continue
            if self.seen[en].get(k, 0) >= c:
                continue
            self.seen[en][k] = c
            self.prog[en].append(("wait", k, c))

    def op(self, en, fn, reads=(), writes=()):
        self._waits(en, reads, writes)
        c = self.cnt.get(en, 0) + 1
        self.cnt[en] = c
        self.prog[en].append(("op", fn, en, 1))
        for r in reads:
            r.r[en] = c
        for w in writes:
            w.w = (en, c)
            w.r = {}
        return (en, c)

    def dma(self, en, fn, semkey, reads=(), writes=()):
        self._waits(en, reads, writes)
        c = self.cnt.get(semkey, 0) + 16
        self.cnt[semkey] = c
        self.prog[en].append(("op", fn, semkey, 16))
        for r in reads:
            r.r[semkey] = c
        for w in writes:
            w.w = (semkey, c)
            w.r = {}
        return (semkey, c)

    def wait_all(self, en, ress):
        self._waits(en, [], ress)


def alias(new, olds):
    for o in olds:
        if o.w is not None:
            k, c = o.w
            if new.r.get(k, 0) < c:
                new.r[k] = c
        for k, c in o.r.items():
            if new.r.get(k, 0) < c:
                new.r[k] = c


PINNED = set()


class Rot:
    def __init__(self, items):
        self.items = items
        self.i = 0

    def next(self):
        for _ in range(len(self.items)):
            it = self.items[self.i % len(self.items)]
            self.i += 1
            if it[1] not in PINNED:
                return it
        raise RuntimeError("all rotating buffers pinned")


def pin(*ress):
    for r in ress:
        PINNED.add(r)


def unpin(*ress):
    for r in ress:
        PINNED.discard(r)


class Builder:
    def __init__(self, cfg):
        self.cfg = cfg
        self.nc = bass.Bass("TRN2", target_bir_lowering=False)
        self.T = Tracker()
        PINNED.clear()
        self.fillq, self.fill_stages, self.fill_rr, self.tiles_left, self.fill_window = [], 0, 0, 1, 2
        self.dram = {}
        self.dma_sems = []

    def din(self, name, shape, dt=F32):
        t = self.nc.dram_tensor(name, list(shape), dt, kind="ExternalInput").ap()
        self.dram[name] = t
        return t

    def new_dma_sem(self):
        k = "dma%d" % len(self.dma_sems)
        self.dma_sems.append(k)
        return k

    def sb(self, name, shape, dt):
        return self.nc.alloc_sbuf_tensor(name, list(shape), dt)

    def build(self):
        nc, T = self.nc, self.T
        cfg = self.cfg
        x_d = self.din("x", [SEQ, D])
        ctx_d = self.din("ctx", [CTX, D])
        vecs_d = self.din("vecs", [NVEC, 128])
        cst_d = self.din("cst", [5, 128, 128])
        w_mod = self.din("w_mod", [DEPTH, D, 9 * D])
        w_in = w_out = None
        if cfg.get("pre_ffn", True) or cfg.get("post_ffn", True):
            w_in = self.din("w_ffn_in", [DEPTH, 2, D, 2 * DFF])
            w_out = self.din("w_ffn_out", [DEPTH, 2, DFF, D])
        out_d = nc.dram_tensor("out", [SEQ, D], F32, kind="ExternalOutput").ap()
        self.w_in, self.w_out, self.w_mod = w_in, w_out, w_mod
        nl = cfg.get("layers", DEPTH)
        if cfg.get("mixer", True):
            self.rope64_d = self.din("rope64", [2, 128, SEQ])
            self.da_w_qkv = self.din("da_w_qkv", [1, D, 3 * D])
            self.da_w_o = self.din("da_w_o", [1, D, D])
            self.lam_d = self.din("lamv", [1, 256])
            self.gqa_w_qkv = self.din("gqa_w_qkv", [1, D, 1536])
            self.gqa_w_o = self.din("gqa_w_o", [1, D, D])
            self.rope32_d = self.din("rope32", [2, 128, SEQ])
            self.mla_w_down = self.din("mla_w_down", [1, D, 416])
            self.mla_w_uq = self.din("mla_w_uq", [1, 256, 1536])
            self.mla_w_ukv = self.din("mla_w_ukv", [1, 128, 2048])
            self.mla_w_o = self.din("mla_w_o", [1, D, D])
            self.na_w_qkv = self.din("na_w_qkv", [1, D, 3 * D])
            self.na_w_o = self.din("na_w_o", [1, D, D])
            self.na_tb = self.din("na_tb", [16, 2, 128, 960])

        self.xT = self.sb("xT", [128, NCH, NTOK], F32)
        self.xres = [Res("x%d" % b) for b in range(5)]
        RBYTES = 85504
        self.R = self.sb("R", [128, RBYTES // 2], BF16)
        self.Rres = Res("R")
        NSLOT = 3
        self.slots = []
        for i in range(NSLOT):
            t = self.sb("wslot%d" % i, [128, 4096], BF16)
            self.slots.append((t, Res("wslot%d" % i), self.new_dma_sem()))
        self.slot_i = 0
        self.banks = []
        for i in range(8):
            t = nc.alloc_psum_tensor("ps%d" % i, [128, 512], F32)
            self.banks.append((t, Res("ps%d" % i)))
        self.ps_all = Rot(self.banks)
        self.ps_acc = Rot(self.banks[0:4])
        self.ps_s = Rot(self.banks[4:8])
        def rot(name, n, shape, dt):
            return Rot([(self.sb("%s%d" % (name, i), shape, dt), Res("%s%d" % (name, i))) for i in range(n)])
        self.sq = rot("sq", 2, [128, 512], BF16)
        self.rstd = rot("rstd", 2, [128, 512], F32)
        self.modw = self.rstd.items[1][0][:].bitcast(BF16).rearrange("p (c n) -> p c n", n=128)
        self.modw_res = Res("modw")
        self.modw_sem = self.new_dma_sem()
        self.rstd = Rot(self.rstd.items[0:1])
        self.tmp = rot("tmp", 3, [128, 512], F32)
        self.sg = rot("sg", 2, [128, 512], F32)
        self.pt = rot("pt", 4, [128, 512], BF16)
        self.qraw = rot("qraw", 2, [128, 512], BF16)
        self.cst = self.sb("cst_sb", [128, 5, 128], BF16)
        self.cst_res = Res("cst")
        self.vecT = self.sb("vecT", [128, NVEC], F32)
        self.vec_res = Res("vecT")
        self.silu_c = self.sb("silu_c", [128, NCH, 2], BF16)
        self.silu_res = Res("silu_c")
        self.modbufs = [(self.sb("modsb%d" % i, [128, 72, 2], F32), Res("modsb%d" % i)) for i in range(2)]
        self.modsb, self.mod_res = self.modbufs[0]
        self.mod_ready = set()
        self.sideq, self.side_cur = [], None
        self.lv = self.sb("lv", [128, 9, NCH, 2], F32)
        self.lv_res = Res("lv")
        self.epsD = self.sb("epsD", [128, 1], F32)
        self.eps64 = self.sb("eps64", [128, 1], F32)
        self.eps128 = self.sb("eps128", [128, 1], F32)
        self.eps_res = Res("eps")

        self.ident = self.cst[:, 0, :]
        self.ones = self.cst[:, 1, :]
        self.blk64 = self.cst[:, 2, :]
        self.perm64 = self.cst[:, 3, :]
        self.perm32 = self.cst[:, 4, :]
        self.ones_wide = self.cst[:].rearrange("p c n -> p (c n)")

        T.dma("pool", lambda e: e.dma_start(out=self.cst[:], in_=cst_d.rearrange("c p n -> p c n")),
              self.new_dma_sem(), writes=[self.cst_res])
        T.op("dve", lambda e: e.memset(self.epsD[:], float(D * EPS)), writes=[self.eps_res])
        T.op("dve", lambda e: e.memset(self.eps64[:], float(64 * EPS)), writes=[self.eps_res])
        T.op("dve", lambda e: e.memset(self.eps128[:], float(128 * EPS)), writes=[self.eps_res])
        self.load_vecs(vecs_d)
        self.load_x(x_d, ctx_d)
        ll = cfg.get("layer_list", list(range(DEPTH)))
        for i, l in enumerate(ll):
            self.layer(l, ll[i + 1] if i + 1 < len(ll) else None)
        self.final(out_d)
        self.emit()
        return nc

    def Rview(self, off_bytes, shape, dt):
        n = int(np.prod(shape[1:]))
        if dt == BF16:
            ap = self.R[:, off_bytes // 2: off_bytes // 2 + n]
        else:
            ap = self.R[:, off_bytes // 2: off_bytes // 2 + 2 * n].bitcast(F32)
        if len(shape) == 3:
            ap = ap.rearrange("p (a b) -> p a b", b=shape[2])
        return ap

    def split3_T(self, src_ap, src_res, nrow, dst_fn, dst_res, stage_bf, stage_res):
        raise NotImplementedError

    def load_vecs(self, vecs_d):
        T = self.T
        st = self.Rview(0, [128, 4, 128], F32)
        hi = self.Rview(2048, [128, 3, 512], BF16)
        r1 = self.Rview(2048 + 3072, [128, 512], F32)
        T.dma("sp", lambda e: e.dma_start(out=st, in_=vecs_d.rearrange("(g p) n -> p g n", p=128)),
              self.new_dma_sem(), writes=[self.Rres])
        stf = st.rearrange("p g n -> p (g n)")
        T.op("dve", lambda e: e.tensor_copy(hi[:, 0, :], stf), reads=[self.Rres], writes=[self.Rres])
        T.op("dve", lambda e: e.tensor_tensor(r1, stf, hi[:, 0, :], ALU.subtract), reads=[self.Rres], writes=[self.Rres])
        T.op("dve", lambda e: e.tensor_copy(hi[:, 1, :], r1), reads=[self.Rres], writes=[self.Rres])
        T.op("dve", lambda e: e.tensor_tensor(r1, r1, hi[:, 1, :], ALU.subtract), reads=[self.Rres], writes=[self.Rres])
        T.op("dve", lambda e: e.tensor_copy(hi[:, 2, :], r1), reads=[self.Rres], writes=[self.Rres])
        for part in range(3):
            bank, bres = self.ps_all.next()
            pb = bank[:].bitcast(BF16)
            def mm(e, part=part, pb=pb):
                ins = None
                for g in range(4):
                    ins = e.transpose(pb[:, g * 128:(g + 1) * 128], hi[:, part, g * 128:(g + 1) * 128], self.ident)
                return ins
            T.op("pe", mm, reads=[self.Rres, self.cst_res], writes=[bres])
            if part == 0:
                T.op("act", lambda e, pb=pb: e.copy(self.vecT[:], pb[:, 0:512]), reads=[bres], writes=[bres, self.vec_res])
            else:
                T.op("dve", lambda e, pb=pb: e.tensor_tensor(self.vecT[:], pb[:, 0:512], self.vecT[:], ALU.add),
                     reads=[bres, self.vec_res], writes=[bres, self.vec_res])
        T.op("act", lambda e: e.activation(self.silu_c[:, :, 0], self.vecT[:, V_C:V_C + 8], AF.Silu),
             reads=[self.vec_res], writes=[self.silu_res])
        T.op("act", lambda e: e.activation(self.silu_c[:, :, 1], self.vecT[:, V_CC:V_CC + 8], AF.Silu),
             reads=[self.vec_res], writes=[self.silu_res])

    def load_x(self, x_d, ctx_d):
        T = self.T
        base = 8192
        stg = [(self.Rview(base + i * 4096, [128, 1024], F32), Res("xstg%d" % i), self.new_dma_sem()) for i in range(2)]
        parts = [(self.Rview(base + 8192 + i * 6144, [128, 3, 1024], BF16), Res("xparts%d" % i)) for i in range(2)]
        r1s = [(self.Rview(base + 8192 + 12288 + i * 4096, [128, 1024], F32), Res("xr1%d" % i)) for i in range(2)]
        for r in [s[1] for s in stg] + [p[1] for p in parts] + [r[1] for r in r1s]:
            alias(r, [self.Rres])
        for j in range(NTOK // 128):
            st, sres, ssem = stg[j % 2]
            pp, pres = parts[j % 2]
            r1, rres = r1s[j % 2]
            src = x_d[j * 128:(j + 1) * 128, :] if j < 16 else ctx_d[(j - 16) * 128:(j - 15) * 128, :]
            T.dma("sp", lambda e, st=st, src=src: e.dma_start(out=st, in_=src), ssem, writes=[sres])
            T.op("act", lambda e, pp=pp, st=st: e.copy(pp[:, 0, :], st), reads=[sres], writes=[pres])
            T.op("dve", lambda e, pp=pp, st=st, r1=r1: e.tensor_tensor(r1, st, pp[:, 0, :], ALU.subtract),
                 reads=[sres, pres], writes=[rres])
            T.op("act", lambda e, pp=pp, r1=r1: e.copy(pp[:, 1, :], r1), reads=[rres], writes=[pres])
            T.op("dve", lambda e, pp=pp, r1=r1: e.tensor_tensor(r1, r1, pp[:, 1, :], ALU.subtract),
                 reads=[rres, pres], writes=[rres])
            T.op("act", lambda e, pp=pp, r1=r1: e.copy(pp[:, 2, :], r1), reads=[rres], writes=[pres])
            blk = min(j // 4, 4)
            xr = self.xres[blk]
            dst = self.xT[:, :, j * 128:(j + 1) * 128]
            for part in range(3):
                bank, bres = self.ps_all.next()
                pb = bank[:].bitcast(BF16)

                def mm(e, part=part, pb=pb, pp=pp):
                    ins = None
                    for c in range(NCH):
                        ins = e.transpose(pb[:, c * 128:(c + 1) * 128], pp[:, part, c * 128:(c + 1) * 128], self.ident)
                    return ins
                T.op("pe", mm, reads=[pres, self.cst_res], writes=[bres])
                pbv = pb.rearrange("p (c n) -> p c n", n=128)
                if part == 0:
                    T.op("act", lambda e, pbv=pbv, dst=dst: e.copy(dst, pbv), reads=[bres], writes=[bres, xr])
                else:
                    T.op("dve", lambda e, pbv=pbv, dst=dst: e.tensor_tensor(dst, pbv, dst, ALU.add),
                         reads=[bres, xr], writes=[bres, xr])
        for r in [s[1] for s in stg] + [p[1] for p in parts] + [r[1] for r in r1s]:
            alias(self.Rres, [r])

    def wslab(self, dmas):
        T = self.T
        t, res, sem = self.slots[self.slot_i % len(self.slots)]
        self.slot_i += 1
        for i, (dst_fn, src) in enumerate(dmas):
            dst = dst_fn(t)
            T.dma("pool", lambda e, dst=dst, src=src: e.dma_start(out=dst, in_=src), sem,
                  writes=[res] if i == 0 else [])
            if i > 0:
                res.w = (sem, T.cnt[sem])
        return t, res

    def g_mod_slab(self, l, s_i, dst, dres, lag):
        T = self.T
        src = self.w_mod[l, :, s_i * 512:(s_i + 1) * 512].rearrange("(c p) n -> p c n", p=128)
        t, res = self.wslab([(lambda t: t[:].rearrange("p (c n) -> p c n", n=512), src)])
        tv = t[:].rearrange("p (c n) -> p c n", n=512)
        bank, bres = self.ps_s.next()
        pm = bank[:, 0:8].rearrange("p (m r) -> p m r", r=2)

        def mm(e):
            ins = None
            for mi in range(4):
                for k in range(NCH):
                    ins = e.matmul(pm[:, mi, :], lhsT=tv[:, k, mi * 128:(mi + 1) * 128], rhs=self.silu_c[:, k, :],
                                   start=(k == 0), stop=(k == NCH - 1))
            return ins
        T.op("pe", mm, reads=[res, self.silu_res], writes=[bres])
        bm = self.vecT[:, V_BM + 72 * l + 4 * s_i: V_BM + 72 * l + 4 * s_i + 4]
        T.op("dve", lambda e: e.tensor_tensor(dst[:, 4 * s_i:4 * s_i + 4, :], pm, bm.unsqueeze(2).broadcast_to([128, 4, 2]), ALU.add),
             reads=[bres, self.vec_res], writes=[bres, dres])
        yield

    def g_mod_chunk(self, l, m, dst, dres, lag):
        T = self.T
        src = self.w_mod[l, :, m * 128:(m + 1) * 128].rearrange("(c p) n -> p c n", p=128)
        T.dma("pool", lambda e: e.dma_start(out=self.modw, in_=src), self.modw_sem, writes=[self.modw_res])
        for _ in range(lag):
            yield
        bank, bres = self.ps_s.next()

        def mm(e):
            ins = None
            for k in range(NCH):
                ins = e.matmul(bank[:, 0:2], lhsT=self.modw[:, k, :], rhs=self.silu_c[:, k, :],
                               start=(k == 0), stop=(k == NCH - 1))
            return ins
        T.op("pe", mm, reads=[self.modw_res, self.silu_res], writes=[bres])
        bm = self.vecT[:, V_BM + 72 * l + m: V_BM + 72 * l + m + 1]
        T.op("dve", lambda e: e.tensor_tensor(dst[:, m, :], bank[:, 0:2], bm.broadcast_to([128, 2]), ALU.add),
             reads=[bres, self.vec_res], writes=[bres, dres])
        yield

    def mod_items(self, l, lag):
        dst, dres = self.modbufs[l % 2]
        if lag:
            return [self.g_mod_chunk(l, m, dst, dres, lag) for m in range(72)]
        return [self.g_mod_slab(l, s_i, dst, dres, lag) for s_i in range(18)]

    def compute_mod(self, l):
        T = self.T
        if l not in self.mod_ready:
            for g in self.mod_items(l, 0):
                for _ in g:
                    pass
            self.mod_ready.add(l)
        self.modsb, self.mod_res = self.modbufs[l % 2]
        md = self.modsb[:].rearrange("p (j c) r -> p j c r", c=NCH)
        for n in range(3):
            g = self.vecT[:, V_NG + (l * 3 + n) * 8: V_NG + (l * 3 + n + 1) * 8]
            T.op("dve", lambda e, n=n: e.tensor_copy(self.lv[:, 3 * n, :, :], md[:, 3 * n, :, :]),
                 reads=[self.mod_res], writes=[self.lv_res])
            T.op("dve", lambda e, n=n: e.tensor_scalar(self.lv[:, 3 * n + 1, :, :], md[:, 3 * n + 1, :, :],
                                                       1.0, float(math.sqrt(D)), ALU.add, ALU.mult),
                 reads=[self.mod_res], writes=[self.lv_res])
            T.op("dve", lambda e, n=n, g=g: e.tensor_tensor(self.lv[:, 3 * n + 1, :, :], self.lv[:, 3 * n + 1, :, :],
                                                            g.unsqueeze(2).broadcast_to([128, NCH, 2]), ALU.mult),
                 reads=[self.lv_res, self.vec_res], writes=[self.lv_res])
            sc = 1.0 if n == 1 else 0.5
            T.op("dve", lambda e, n=n, sc=sc: e.tensor_scalar(self.lv[:, 3 * n + 2, :, :], md[:, 3 * n + 2, :, :],
                                                              sc, None, ALU.mult),
                 reads=[self.mod_res], writes=[self.lv_res])

    def rms_rstd(self, src_fn, nch, ntok, reads, ones_ap=None, np_=128):
        T = self.T
        bank, bres = self.ps_s.next()
        sqs = []
        for c in range(nch):
            sq, sres = self.sq.next()
            T.op("act", lambda e, sq=sq, c=c: e.activation(sq[0:np_, 0:ntok], src_fn(c), AF.Square), reads=reads, writes=[sres])
            T.op("pe", lambda e, sq=sq, c=c: e.matmul(bank[0:np_, 0:ntok], lhsT=self.ones[0:np_, 0:np_],
                                                     rhs=sq[0:np_, 0:ntok], start=(c == 0), stop=(c == nch - 1)),
                 reads=[sres, self.cst_res], writes=[bres])
        rs, rres = self.rstd.next()
        return bank, bres, rs, rres

    def modnorm(self, l, n, blocks, dst_fn, dst_res_fn):
        T = self.T
        for b in blocks:
            t0, nt, isctx = BLOCKS[b]
            r = 1 if isctx else 0
            xr = self.xres[b]
            bank, bres, rs, rres = self.rms_rstd(lambda c, t0=t0, nt=nt: self.xT[:, c, t0:t0 + nt], NCH, nt, [xr])
            self.act_rsqrt(rs[:, 0:nt], bank[:, 0:nt], self.epsD[:, 0:1], [bres, self.eps_res], [bres, rres])
            dres = dst_res_fn(b)
            for c in range(NCH):
                tm, tres = self.tmp.next()
                gs = self.lv[:, 3 * n + 1, c, r:r + 1]
                sh = self.lv[:, 3 * n, c, r:r + 1]
                T.op("dve", lambda e, tm=tm, c=c, gs=gs, rs=rs, nt=nt, t0=t0: e.scalar_tensor_tensor(
                    tm[:, 0:nt], self.xT[:, c, t0:t0 + nt], gs, rs[:, 0:nt], ALU.mult, ALU.mult),
                    reads=[xr, rres, self.lv_res], writes=[tres])
                T.op("act", lambda e, tm=tm, c=c, sh=sh, nt=nt, b=b: e.activation(dst_fn(b, c), tm[:, 0:nt], AF.Identity, bias=sh),
                     reads=[tres, self.lv_res], writes=[dres])

    def ffn(self, l, j, with_ctx):
        T = self.T
        n = 0 if j == 0 else 2
        HB = 18432
        passes = [[0, 1], [2, 3]]
        for pi, pblocks in enumerate(passes):
            hh = self.Rview(0, [128, NCH, 1152], BF16)
            uu = self.Rview(HB, [128, NFF, 1152], BF16)
            hres = [Res("hh%d" % i) for i in range(3)]
            ures = [Res("uu%d" % i) for i in range(3)]
            for r_ in hres + ures:
                alias(r_, [self.Rres])
            subs = [(BLOCKS[b][0], 512, i * 512, self.xres[b], False) for i, b in enumerate(pblocks)]
            if with_ctx:
                subs.append((2048 + pi * 128, 128, 1024, self.xres[4], True))
            for si, (t0, nt, lo, xr, isctx) in enumerate(subs):
                r = 1 if isctx else 0
                bank, bres, rs, rres = self.rms_rstd(lambda c, t0=t0, nt=nt: self.xT[:, c, t0:t0 + nt], NCH, nt, [xr])
                self.act_rsqrt(rs[:, 0:nt], bank[:, 0:nt], self.epsD[:, 0:1], [bres, self.eps_res], [bres, rres])
                for c in range(NCH):
                    tm, tres = self.tmp.next()
                    gs = self.lv[:, 3 * n + 1, c, r:r + 1]
                    sh = self.lv[:, 3 * n, c, r:r + 1]
                    T.op("dve", lambda e, tm=tm, c=c, gs=gs, rs=rs, nt=nt, t0=t0: e.scalar_tensor_tensor(
                        tm[:, 0:nt], self.xT[:, c, t0:t0 + nt], gs, rs[:, 0:nt], ALU.mult, ALU.mult),
                        reads=[xr, rres, self.lv_res], writes=[tres])
                    T.op("act", lambda e, tm=tm, c=c, sh=sh, nt=nt, lo=lo: e.activation(hh[:, c, lo:lo + nt], tm[:, 0:nt], AF.Identity, bias=sh),
                         reads=[tres, self.lv_res], writes=[hres[si]])
            for s in range(NFF // 2):
                srcg = self.w_in[l, j, :, s * 256:(s + 1) * 256].rearrange("(c p) n -> p c n", p=128)
                srcu = self.w_in[l, j, :, DFF + s * 256:DFF + (s + 1) * 256].rearrange("(c p) n -> p c n", p=128)
                t, res = self.wslab([
                    (lambda t: t[:].rearrange("p (c n) -> p c n", n=512)[:, :, 0:256], srcg),
                    (lambda t: t[:].rearrange("p (c n) -> p c n", n=512)[:, :, 256:512], srcu)])
                tv = t[:].rearrange("p (c n) -> p c n", n=512)
                for i in range(2):
                    f = 2 * s + i
                    for si, (t0, nt, lo, xr, isctx) in enumerate(subs):
                        gb, gres = self.ps_all.next()
                        ub, ures_ = self.ps_all.next()

                        def mm(e, ob, col, lo=lo, nt=nt, tv=tv):
                            ins = None
                            for k in range(NCH):
                                ins = e.matmul(ob[:, 0:nt], lhsT=tv[:, k, col:col + 128], rhs=hh[:, k, lo:lo + nt],
                                               start=(k == 0), stop=(k == NCH - 1))
                            return ins
                        T.op("pe", lambda e, gb=gb, i=i, mm=mm: mm(e, gb, i * 128), reads=[res, hres[si]], writes=[gres])
                        T.op("pe", lambda e, ub=ub, i=i, mm=mm: mm(e, ub, 256 + i * 128), reads=[res, hres[si]], writes=[ures_])
                        sgt, sgres = self.sg.next()
                        T.op("act", lambda e, gb=gb, sgt=sgt, nt=nt: e.activation(sgt[:, 0:nt], gb[:, 0:nt], AF.Silu),
                             reads=[gres], writes=[gres, sgres])
                        T.op("dve", lambda e, ub=ub, sgt=sgt, nt=nt, lo=lo, f=f: e.tensor_tensor(
                            uu[:, f, lo:lo + nt], ub[:, 0:nt], sgt[:, 0:nt], ALU.mult),
                            reads=[ures_, sgres], writes=[ures_, ures[si]])
            for m in range(NCH):
                src = self.w_out[l, j, :, m * 128:(m + 1) * 128].rearrange("(c p) n -> p c n", p=128)
                t, res = self.wslab([
                    (lambda t: t[:, 0:NFF * 128].rearrange("p (c n) -> p c n", n=128)[:, 0:11, :], src[:, 0:11, :]),
                    (lambda t: t[:, 0:NFF * 128].rearrange("p (c n) -> p c n", n=128)[:, 11:22, :], src[:, 11:22, :])])
                tv = t[:, 0:NFF * 128].rearrange("p (c n) -> p c n", n=128)
                for si, (t0, nt, lo, xr, isctx) in enumerate(subs):
                    r = 1 if isctx else 0
                    yb, yres = self.ps_all.next()

                    def mm(e, yb=yb, lo=lo, nt=nt, tv=tv):
                        ins = None
                        for k in range(NFF):
                            ins = e.matmul(yb[:, 0:nt], lhsT=tv[:, k, :], rhs=uu[:, k, lo:lo + nt],
                                           start=(k == 0), stop=(k == NFF - 1))
                        return ins
                    T.op("pe", mm, reads=[res, ures[si]], writes=[yres])
                    gh = self.lv[:, 3 * n + 2, m, r:r + 1]
                    T.op("dve", lambda e, yb=yb, gh=gh, m=m, t0=t0, nt=nt: e.scalar_tensor_tensor(
                        self.xT[:, m, t0:t0 + nt], yb[:, 0:nt], gh, self.xT[:, m, t0:t0 + nt], ALU.mult, ALU.add),
                        reads=[yres, self.lv_res, xr], writes=[yres, xr])
            for r_ in hres + ures:
                alias(self.Rres, [r_])

    def layer(self, l, nxt=None):
        cfg = self.cfg
        with_ctx = l < DEPTH - 1
        self.compute_mod(l)
        if cfg.get("pre_ffn", True):
            self.ffn(l, 0, True)
        if cfg.get("mixer", True):
            if nxt is not None and cfg.get("mod_prefetch", True):
                self.sideq = self.mod_items(nxt, 8)
                self.mod_ready.add(nxt)
            self.mixer(l, with_ctx)
            self.side_finish(all_=True)
        if cfg.get("post_ffn", True):
            self.ffn(l, 1, with_ctx)

    def attn_setup(self, rope_d=None):
        T = self.T
        A = {}
        A["hT"] = self.Rview(0, [128, NCH, NTOK], BF16)
        A["hres"] = [Res("hT%d" % b) for b in range(5)]
        A["ropeC"] = self.Rview(36864, [128, SEQ], F32)
        A["ropeS"] = self.Rview(45056, [128, SEQ], F32)
        A["rope_res"] = Res("rope")
        A["QT"], A["KT"], A["V"] = [], [], []
        A["Qres"], A["Kres"], A["Vres"] = [], [], []
        for s_ in range(2):
            o = 53248 + s_ * 13824
            A["QT"].append(self.Rview(o, [128, NTOK], BF16))
            A["KT"].append(self.Rview(o + 4608, [128, NTOK], BF16))
            A["V"].append(self.Rview(o + 9216, [128, 18, 128], BF16))
            A["Qres"].append([Res("Q%d_%d" % (s_, b)) for b in range(5)])
            A["Kres"].append([Res("K%d_%d" % (s_, b)) for b in range(5)])
            A["Vres"].append([Res("V%d_%d" % (s_, b)) for b in range(5)])
        A["Oh"] = self.Rview(80896, [128, NTOK], BF16)
        A["Ores"] = [Res("Oh%d" % b) for b in range(5)]
        allres = A["hres"] + [A["rope_res"]] + A["Ores"]
        for s_ in range(2):
            allres += A["Qres"][s_] + A["Kres"][s_] + A["Vres"][s_]
        for r_ in allres:
            alias(r_, [self.Rres])
        A["allres"] = allres
        if rope_d is not None:
            sem = self.new_dma_sem()
            T.dma("sp", lambda e: e.dma_start(out=A["ropeC"], in_=rope_d[0]), sem, writes=[A["rope_res"]])
            T.dma("sp", lambda e: e.dma_start(out=A["ropeS"], in_=rope_d[1]), sem, writes=[])
            A["rope_res"].w = (sem, T.cnt[sem])
        return A

    def attn_done(self, A):
        for r_ in A["allres"]:
            alias(self.Rres, [r_])

    def proj_fm(self, wv, wres, col0, ncol, A, b, K_chunks=NCH, rhs_fn=None, rhs_res=None):
        T = self.T
        t0, nt, _ = BLOCKS[b]
        bank, bres = self.ps_s.next()
        if rhs_fn is None:
            rhs_fn = lambda k: A["hT"][:, k, t0:t0 + nt]
            rhs_res = A["hres"][b]

        def mm(e):
            ins = None
            for k in range(K_chunks):
                ins = e.matmul(bank[0:ncol, 0:nt], lhsT=wv[:, k, col0:col0 + ncol], rhs=rhs_fn(k),
                               start=(k == 0), stop=(k == K_chunks - 1))
            return ins
        T.op("pe", mm, reads=[wres, rhs_res], writes=[bres])
        return bank, bres

    def rope_to(self, bank, bres, A, b, dst, dres, p0=0, np_=128, perm=None, lag=False):
        T = self.T
        t0, nt, _ = BLOCKS[b]
        perm = self.perm64 if perm is None else perm
        P = slice(p0, p0 + np_)
        qr, qres = self.qraw.next()
        T.op("act", lambda e: e.copy(qr[P, 0:nt], bank[P, 0:nt]), reads=[bres], writes=[bres, qres])
        t1, t1res = self.tmp.next()
        T.op("dve", lambda e: e.tensor_tensor(t1[P, 0:nt], bank[P, 0:nt], A["ropeC"][P, t0:t0 + nt], ALU.mult),
             reads=[bres, A["rope_res"]], writes=[bres, t1res])

        def stage2():
            b2, b2res = self.ps_s.next()
            T.op("pe", lambda e: e.matmul(b2[P, 0:nt], lhsT=perm[P, P], rhs=qr[P, 0:nt], start=True, stop=True),
                 reads=[qres, self.cst_res], writes=[b2res])
            t2, t2res = self.tmp.next()
            T.op("dve", lambda e: e.tensor_tensor(t2[P, 0:nt], b2[P, 0:nt], A["ropeS"][P, t0:t0 + nt], ALU.mult),
                 reads=[b2res, A["rope_res"]], writes=[b2res, t2res])
            T.op("dve", lambda e: e.tensor_tensor(dst[P, t0:t0 + nt], t1[P, 0:nt], t2[P, 0:nt], ALU.add),
                 reads=[t1res, t2res], writes=[dres])
        if lag:
            pin(qres, t1res)

            def stage2_unpin():
                stage2()
                unpin(qres, t1res)
            return stage2_unpin
        stage2()
        return None

    def proj_v(self, wv, wres, col0, ncol, A, b, dstV, dres, dcol0=0, lhs_fn=None, lhs_res=None, K_chunks=NCH):
        T = self.T
        t0, nt, _ = BLOCKS[b]
        ntile = nt // 128
        bank, bres = self.ps_s.next()
        if lhs_fn is None:
            lhs_fn = lambda k, j: A["hT"][:, k, t0 + j * 128: t0 + (j + 1) * 128]
            lhs_res = A["hres"][b]
        bv = bank[:, 0:ntile * ncol].rearrange("p (j n) -> p j n", n=ncol)

        def mm(e):
            ins = None
            for j in range(ntile):
                for k in range(K_chunks):
                    ins = e.matmul(bv[:, j, :], lhsT=lhs_fn(k, j), rhs=wv[:, k, col0:col0 + ncol],
                                   start=(k == 0), stop=(k == K_chunks - 1))
            return ins
        T.op("pe", mm, reads=[wres, lhs_res], writes=[bres])
        T.op("dve", lambda e: e.tensor_copy(dstV[:, t0 // 128: t0 // 128 + ntile, dcol0:dcol0 + ncol], bv),
             reads=[bres], writes=[bres, dres])

    def attn_scores_pv(self, A, KT, Kres, kp, QT, Qres_b, qp, V, Vres, q0, nq, ktiles, scale, sep_den,
                       k2=None, bias_fn=None):
        T = self.T
        accO, oRes = self.ps_acc.next()
        if sep_den:
            accD, dRes = self.ps_acc.next()
        else:
            accD, dRes = None, None
        M = V.shape[2]
        n = len(ktiles)
        G = 2
        groups = [list(range(i, min(i + G, n))) for i in range(0, n, G)]
        ng = len(groups)
        live = {}

        def stage_s(gi):
            idxs = groups[gi]
            banks = [self.ps_s.next() for _ in idxs]
            kbs = [(ktiles[i] // 4 if ktiles[i] < 16 else 4) for i in idxs]

            def mm(e):
                ins = None
                for (sb_, sres), i in zip(banks, idxs):
                    kt = ktiles[i]
                    ins = e.matmul(sb_[:, 0:nq], lhsT=KT[kp, kt * 128:(kt + 1) * 128], rhs=QT[qp, q0:q0 + nq],
                                   start=True, stop=True)
                return ins
            T.op("pe", mm, reads=[Kres[kb] for kb in set(kbs)] + [Qres_b], writes=[b_[1] for b_ in banks])
            pts = []
            for (sb_, sres), i in zip(banks, idxs):
                kt = ktiles[i]
                pt_, ptres = self.pt.next()
                if bias_fn is None:
                    T.op("act", lambda e, sb_=sb_, pt_=pt_: e.activation(pt_[:, 0:nq], sb_[:, 0:nq], AF.Exp, scale=float(scale)),
                         reads=[sres], writes=[sres, ptres])
                else:
                    bias_fn(sb_, sres, pt_, ptres, kt, i)
                pts.append((pt_, ptres))
            live[gi] = (pts, kbs)

        def stage_pv(gi):
            idxs = groups[gi]
            pts, kbs = live.pop(gi)

            def mm2(e):
                ins = None
                for (pt_, ptres), i in zip(pts, idxs):
                    kt = ktiles[i]
                    ins = e.matmul(accO[0:M, 0:nq], lhsT=V[:, kt, :], rhs=pt_[:, 0:nq], start=(i == 0), stop=(i == n - 1))
                    if sep_den:
                        ins = e.matmul(accD[:, 0:nq], lhsT=self.ones, rhs=pt_[:, 0:nq], start=(i == 0), stop=(i == n - 1))
                dn = self.cfg.get("dummy_n", 0)
                if dn:
                    for _ in range(self.cfg.get("dummy_k", 1)):
                        e.matmul(self.banks[7][0][:, 0:dn], lhsT=self.ones, rhs=self.ones_wide[:, 0:dn], start=True, stop=True)
                return ins
            T.op("pe", mm2, reads=[p_[1] for p_ in pts] + [Vres[kb] for kb in set(kbs)] + [self.cst_res],
                 writes=[oRes] + ([dRes] if sep_den else []))
        for gi in range(ng + 1):
            if gi < ng:
                stage_s(gi)
                for _ in groups[gi]:
                    self.fill_tick()
            if gi - 1 >= 0:
                stage_pv(gi - 1)
        return accO, oRes, accD, dRes

    def wo_accum(self, wo, wres, A, l, b, q0, nq, rows=128):
        T = self.T
        isctx = BLOCKS[b][2]
        r = 1 if isctx else 0
        xr = self.xres[b]
        for m in range(NCH):
            yb, yres = self.ps_s.next()
            T.op("pe", lambda e, yb=yb, m=m: e.matmul(yb[:, 0:nq], lhsT=wo[0:rows, m * 128:(m + 1) * 128],
                                                     rhs=A["Oh"][0:rows, q0:q0 + nq], start=True, stop=True),
                 reads=[wres, A["Ores"][b]], writes=[yres])
            g5 = self.lv[:, 5, m, r:r + 1]
            T.op("dve", lambda e, yb=yb, m=m, g5=g5: e.scalar_tensor_tensor(
                self.xT[:, m, q0:q0 + nq], yb[:, 0:nq], g5, self.xT[:, m, q0:q0 + nq], ALU.mult, ALU.add),
                reads=[yres, self.lv_res, xr], writes=[yres, xr])

    def enqueue(self, gen, est, tag="proj"):
        self.fillq.append([gen, est, tag])
        self.fill_stages += est

    def fill_one(self):
        if not self.fillq:
            return False
        idx = self.fill_rr % min(self.fill_window, len(self.fillq))
        self.fill_rr += 1
        ent = self.fillq[idx]
        try:
            next(ent[0])
            ent[1] = max(1, ent[1] - 1)
            self.fill_stages = max(len(self.fillq), self.fill_stages - 1)
        except StopIteration:
            del self.fillq[idx]
            self.fill_stages = max(len(self.fillq), self.fill_stages - ent[1])
        return True

    def side_tick(self):
        if self.side_cur is None and self.sideq:
            self.side_cur = self.sideq.pop(0)
        if self.side_cur is not None:
            try:
                next(self.side_cur)
            except StopIteration:
                self.side_cur = None

    def side_finish(self, all_=False):
        while self.side_cur is not None or (all_ and self.sideq):
            self.side_tick()

    def fill_tick(self):
        self.side_tick()
        n = -(-self.fill_stages // max(1, self.tiles_left))
        self.tiles_left = max(1, self.tiles_left - 1)
        for _ in range(n):
            if not self.fill_one():
                break

    def drain(self, keep_tag=None):
        keep = [e for e in self.fillq if keep_tag is not None and e[2] == keep_tag]
        run = [e for e in self.fillq if not (keep_tag is not None and e[2] == keep_tag)]
        self.fillq = run
        self.fill_stages = sum(e[1] for e in run)
        while self.fillq:
            self.fill_one()
        self.fillq = keep
        self.fill_stages = sum(e[1] for e in keep)

    def run_units(self, n_units, prep, attn, tiles_of):
        self.fillq, self.fill_stages, self.fill_rr, self.tiles_left = [], 0, 0, 1
        for g, est in prep(0):
            for _ in g:
                pass
        for u in range(n_units):
            if u + 1 < n_units:
                for g, est in prep(u + 1):
                    self.enqueue(g, est, "proj")
            self.tiles_left = tiles_of(u)
            attn(u)
            self.side_finish()
            self.drain(keep_tag="wo")
        self.drain()

    def g_wo(self, wo, wres, A, l, b, q0, nq, rows=128):
        T = self.T
        isctx = BLOCKS[b][2]
        r = 1 if isctx else 0
        xr = self.xres[b]
        for m in range(NCH):
            yb, yres = self.ps_s.next()
            T.op("pe", lambda e, yb=yb, m=m: e.matmul(yb[:, 0:nq], lhsT=wo[0:rows, m * 128:(m + 1) * 128],
                                                     rhs=A["Oh"][0:rows, q0:q0 + nq], start=True, stop=True),
                 reads=[wres, A["Ores"][b]], writes=[yres])
            g5 = self.lv[:, 5, m, r:r + 1]
            T.op("dve", lambda e, yb=yb, m=m, g5=g5: e.scalar_tensor_tensor(
                self.xT[:, m, q0:q0 + nq], yb[:, 0:nq], g5, self.xT[:, m, q0:q0 + nq], ALU.mult, ALU.add),
                reads=[yres, self.lv_res, xr], writes=[yres, xr])
            if m % 2 == 1:
                yield

    def g_call(self, fn):
        fn()
        yield

    def g_head64(self, wv, wres, col0, A, b, dst, dres, gvec, rope, norm, np_=64):
        T = self.T
        t0, nt, isctx = BLOCKS[b]
        P = slice(0, np_)
        bank, bres = self.proj_fm(wv, wres, col0, np_, A, b)
        if norm:
            sq, sres = self.sq.next()
            T.op("act", lambda e: e.activation(sq[P, 0:nt], bank[P, 0:nt], AF.Square), reads=[bres], writes=[sres])
            pin(bres, sres)
            yield
            sb_, sbres = self.ps_s.next()
            T.op("pe", lambda e: e.matmul(sb_[P, 0:nt], lhsT=self.blk64[P, P], rhs=sq[P, 0:nt], start=True, stop=True),
                 reads=[sres, self.cst_res], writes=[sbres])
            rs, rres = self.rstd.next()
            self.act_rsqrt(rs[P, 0:nt], sb_[P, 0:nt], self.eps64[P, 0:1], [sbres, self.eps_res], [sbres, rres])
            unpin(bres, sres)
            if rope and not isctx:
                qn, qnres = self.sg.next()
                T.op("dve", lambda e: e.scalar_tensor_tensor(qn[P, 0:nt], bank[P, 0:nt], gvec, rs[P, 0:nt], ALU.mult, ALU.mult),
                     reads=[bres, rres, self.vec_res, self.gq_res], writes=[bres, qnres])
                st2 = self.rope_to(qn, qnres, A, b, dst, dres, p0=0, np_=np_, lag=True)
                yield
                st2()
            else:
                T.op("dve", lambda e: e.scalar_tensor_tensor(dst[P, t0:t0 + nt], bank[P, 0:nt], gvec, rs[P, 0:nt], ALU.mult, ALU.mult),
                     reads=[bres, rres, self.vec_res, self.gq_res], writes=[bres, dres])
        else:
            if rope and not isctx:
                st2 = self.rope_to(bank, bres, A, b, dst, dres, p0=0, np_=np_, lag=True)
                yield
                st2()
            else:
                T.op("act", lambda e: e.copy(dst[P, t0:t0 + nt], bank[P, 0:nt]), reads=[bres], writes=[bres, dres])
        yield

    def act_rsqrt(self, dst, src, eps_ap, reads, writes):
        T = self.T
        T.op("act", lambda e: e.activation(dst, src, AF.Ln, bias=eps_ap), reads=reads, writes=writes)
        T.op("act", lambda e: e.activation(dst, dst, AF.Exp, scale=-0.5), reads=[writes[-1]], writes=[writes[-1]])

    def act_recip(self, dst, src, reads, writes):
        T = self.T
        T.op("act", lambda e: e.activation(dst, src, AF.Ln), reads=reads, writes=writes)
        T.op("act", lambda e: e.activation(dst, dst, AF.Exp, scale=-1.0), reads=[writes[-1]], writes=[writes[-1]])

    def set_pools(self, nacc):
        self.ps_acc = Rot(self.banks[0:nacc])
        self.ps_s = Rot(self.banks[nacc:(7 if self.cfg.get("dummy_n", 0) else 8)])

    def flush_pending(self):
        for f in self.pending:
            f()
        self.pending = []

    def mixer(self, l, with_ctx):
        kind = l % 4
        self.pending = []
        self.fillq, self.fill_stages, self.fill_rr, self.tiles_left = [], 0, 0, 1
        self.fill_window = 1 if kind == 0 else 2
        self.set_pools(3 if kind == 0 else 2)
        if kind == 0:
            self.mixer_da(l, with_ctx)
        elif kind == 1:
            self.mixer_gqa(l, with_ctx)
        elif kind == 2:
            self.mixer_mla(l, with_ctx)
        else:
            self.mixer_na(l, with_ctx)

    def head64_qk(self, wv, wres, col0, A, b, dst, dres, gvec, rope, norm, src_rhs=None):
        T = self.T
        t0, nt, isctx = BLOCKS[b]
        P = slice(0, 64)
        bank, bres = self.proj_fm(wv, wres, col0, 64, A, b)
        if norm:
            sb_, sbres, rs, rres = self.rms_rstd(lambda c: bank[P, 0:nt], 1, nt, [bres], np_=64)
            self.act_rsqrt(rs[P, 0:nt], sb_[P, 0:nt], self.eps64[P, 0:1], [sbres, self.eps_res], [sbres, rres])
            if rope and not isctx:
                qn, qnres = self.sg.next()
                T.op("dve", lambda e: e.scalar_tensor_tensor(qn[P, 0:nt], bank[P, 0:nt], gvec, rs[P, 0:nt], ALU.mult, ALU.mult),
                     reads=[bres, rres, self.vec_res, self.gq_res], writes=[bres, qnres])
                self.rope_to(qn, qnres, A, b, dst, dres, p0=0, np_=64)
            else:
                T.op("dve", lambda e: e.scalar_tensor_tensor(dst[P, t0:t0 + nt], bank[P, 0:nt], gvec, rs[P, 0:nt], ALU.mult, ALU.mult),
                     reads=[bres, rres, self.vec_res, self.gq_res], writes=[bres, dres])
        else:
            if rope and not isctx:
                self.rope_to(bank, bres, A, b, dst, dres, p0=0, np_=64)
            else:
                T.op("act", lambda e: e.copy(dst[P, t0:t0 + nt], bank[P, 0:nt]), reads=[bres], writes=[bres, dres])

    def norm_o64(self, A, accO, oRes, b, q0, nq, half=0):
        T = self.T
        rd, rdres = self.rstd.next()
        self.act_recip(rd[0:64, 0:nq], accO[64:128, 0:nq], [oRes], [oRes, rdres])
        T.op("dve", lambda e: e.tensor_tensor(A["Oh"][64 * half:64 * half + 64, q0:q0 + nq], accO[0:64, 0:nq], rd[0:64, 0:nq], ALU.mult),
             reads=[oRes, rdres, A["Ores"][b]], writes=[oRes, A["Ores"][b]])

    def set_v_ones(self, A):
        T = self.T
        for s_ in range(2):
            T.op("dve", lambda e, s_=s_: e.memset(A["V"][s_][:, :, 64:128], 1.0), writes=A["Vres"][s_])

    def mixer_gqa(self, l, with_ctx):
        T = self.T
        A = self.attn_setup(self.rope64_d)
        self.set_v_ones(A)
        self.gq = self.sb("gq_l%d" % l, [128, 2], F32)
        self.gq_res = Res("gq")
        T.op("dve", lambda e: e.tensor_scalar(self.gq[:, 0:1], self.vecT[:, V_QN:V_QN + 1], 8.0, None, ALU.mult),
             reads=[self.vec_res], writes=[self.gq_res])
        T.op("dve", lambda e: e.tensor_scalar(self.gq[:, 1:2], self.vecT[:, V_KN:V_KN + 1], 8.0, None, ALU.mult),
             reads=[self.vec_res], writes=[self.gq_res])
        blocks = [0, 1, 2, 3, 4]
        self.modnorm(l, 1, blocks, lambda b, c: A["hT"][:, c, BLOCKS[b][0]:BLOCKS[b][0] + BLOCKS[b][1]], lambda b: A["hres"][b])
        qblocks = [0, 1, 2, 3] + ([4] if with_ctx else [])
        W = {}

        def prep(u):
            kvh, j = u // 2, u % 2
            ks, qs = kvh % 2, u % 2
            items = []
            if j == 0:
                wq = self.gqa_w_qkv[0, :, kvh * 256:(kvh + 1) * 256].rearrange("(c p) n -> p c n", p=128)
                wk = self.gqa_w_qkv[0, :, 1024 + kvh * 64:1024 + (kvh + 1) * 64].rearrange("(c p) n -> p c n", p=128)
                wvv = self.gqa_w_qkv[0, :, 1280 + kvh * 64:1280 + (kvh + 1) * 64].rearrange("(c p) n -> p c n", p=128)
                vq = lambda t: t[:, 0:2048].rearrange("p (c n) -> p c n", n=256)
                vk = lambda t: t[:, 2048:3072].rearrange("p (c n) -> p c n", n=128)
                vv = lambda t: t[:, 3072:3584].rearrange("p (c n) -> p c n", n=64)
                t, wres = self.wslab([(vq, wq), (lambda t: vk(t)[:, :, 0:64], wk), (lambda t: vk(t)[:, :, 64:128], wk), (vv, wvv)])
                wo_src = self.gqa_w_o[0, kvh * 256:(kvh + 1) * 256, :].rearrange("(g p) n -> p g n", p=128)
                vo = lambda t: t[:, 0:2048].rearrange("p (g n) -> p g n", n=1024)
                t2, wres2 = self.wslab([(vo, wo_src)])
                W[kvh] = (vq(t), vk(t), vv(t), wres, vo(t2), wres2)
                tq, tk, tv, wres, two, wres2 = W[kvh]
                for b in blocks:
                    items.append((self.g_head64(tk, wres, 0, A, b, A["KT"][ks], A["Kres"][ks][b], self.gq[:, 1:2], True, True, np_=128), 3))
                    items.append((self.g_call(lambda b=b, tv=tv, wres=wres, ks=ks: self.proj_v(tv, wres, 0, 64, A, b, A["V"][ks], A["Vres"][ks][b])), 1))
            tq, tk, tv, wres, two, wres2 = W[kvh]
            for b in qblocks:
                items.append((self.g_head64(tq, wres, j * 128, A, b, A["QT"][qs], A["Qres"][qs][b], self.gq[:, 0:1], True, True, np_=128), 3))
            return items

        def attn(u):
            kvh, j = u // 2, u % 2
            ks, qs = kvh % 2, u % 2
            tq, tk, tv, wres, two, wres2 = W[kvh]
            KT, V, QT = A["KT"][ks], A["V"][ks], A["QT"][qs]
            for a in range(2):
                P = slice(64 * a, 64 * a + 64)
                for b in qblocks:
                    q0, nq, isctx = BLOCKS[b]
                    ktiles = list(range(18)) if not isctx else [16, 17]
                    accO, oRes, _, _ = self.attn_scores_pv(A, KT, A["Kres"][ks], P, QT, A["Qres"][qs][b], P,
                                                           V, A["Vres"][ks], q0, nq, ktiles, 0.125, False)
                    self.norm_o64(A, accO, oRes, b, q0, nq, half=a)
                    if a == 1:
                        self.enqueue(self.g_wo(two[:, j, :], wres2, A, l, b, q0, nq, rows=128), 4, "wo")
        ntile = 2 * (4 * 18 + (2 if with_ctx else 0))
        self.run_units(8, prep, attn, lambda u: ntile)
        self.attn_done(A)

    def mixer_na(self, l, with_ctx):
        T = self.T
        A = self.attn_setup(None)
        self.set_v_ones(A)
        self.gq_res = Res("gq")
        blocks = [0, 1, 2, 3, 4]
        self.modnorm(l, 1, blocks, lambda b, c: A["hT"][:, c, BLOCKS[b][0]:BLOCKS[b][0] + BLOCKS[b][1]], lambda b: A["hres"][b])
        TB = [self.Rview(36864 + i * 7680, [128, 2, 960], F32) for i in range(2)]
        TBres = [Res("tb%d" % i) for i in range(2)]
        TBsem = [self.new_dma_sem() for i in range(2)]
        for r_ in TBres:
            alias(r_, [A["rope_res"]])
        W = {}

        def prep(h):
            s_ = h % 2
            wq = self.na_w_qkv[0, :, h * 64:(h + 1) * 64].rearrange("(c p) n -> p c n", p=128)
            wk = self.na_w_qkv[0, :, D + h * 64:D + (h + 1) * 64].rearrange("(c p) n -> p c n", p=128)
            wvv = self.na_w_qkv[0, :, 2 * D + h * 64:2 * D + (h + 1) * 64].rearrange("(c p) n -> p c n", p=128)
            v3 = lambda t, i: t[:, i * 512:(i + 1) * 512].rearrange("p (c n) -> p c n", n=64)
            dl = [(lambda t: v3(t, 0), wq), (lambda t: v3(t, 1), wk), (lambda t: v3(t, 2), wvv)]
            if h % 2 == 1:
                dl.append((lambda t: t[:, 2048:3072], self.na_w_o[0, (h - 1) * 64:(h + 1) * 64, :]))
            t, wres = self.wslab(dl)
            tq, tk, tv, two = v3(t, 0), v3(t, 1), v3(t, 2), t[:, 2048:3072]
            tb, tbres, tbsem = TB[s_], TBres[s_], TBsem[s_]
            T.dma("sp", lambda e: e.dma_start(out=tb, in_=self.na_tb[h].rearrange("t p n -> p t n")), tbsem, writes=[tbres])
            W[h] = (two, wres, tb, tbres)
            QT, KT, V = A["QT"][s_], A["KT"][s_], A["V"][s_]
            items = []
            for b in blocks:
                if b < 4:
                    items.append((self.g_head64(tq, wres, 0, A, b, QT, A["Qres"][s_][b], None, False, False), 1))
                items.append((self.g_head64(tk, wres, 0, A, b, KT, A["Kres"][s_][b], None, False, False), 1))
                items.append((self.g_call(lambda b=b: self.proj_v(tv, wres, 0, 64, A, b, V, A["Vres"][s_][b])), 1))
            return items

        def attn(h):
            s_ = h % 2
            two, wres, tb, tbres = W[h]
            for b in range(4):
                for hb in range(2):
                    self.na_halfblock(A, s_, b, hb, tb, tbres, half=h % 2)
                q0, nq, _ = BLOCKS[b]
                if h % 2 == 1:
                    self.enqueue(self.g_wo(two, wres, A, l, b, q0, nq, rows=128), 4, "wo")
        self.run_units(16, prep, attn, lambda u: 60)
        self.attn_done(A)

    def na_halfblock(self, A, s_, b, hb, tb, tbres, half=0):
        T = self.T
        j = 2 * b + hb
        q0, nq = j * 256, 256
        QT, KT, V = A["QT"][s_], A["KT"][s_], A["V"][s_]
        if j == 0:
            kts, full = [0, 1, 2, 3], True
        elif j == 7:
            kts, full = [12, 13, 14, 15], True
        else:
            kts, full = list(range(2 * j - 2, 2 * j + 4)), False
        tsel = 0 if full else 1
        ktiles = kts + [16, 17]
        P = slice(0, 64)

        def bias_fn(sb_, sres, pt_, ptres, kt, i):
            if kt >= 16:
                T.op("act", lambda e: e.activation(pt_[:, 0:nq], sb_[:, 0:nq], AF.Exp, scale=0.125),
                     reads=[sres], writes=[sres, ptres])
                return
            a0 = 4 * j - 2 * kt + 7
            tm, tres = self.tmp.next()
            T.op("dve", lambda e: e.scalar_tensor_tensor(tm[:, 0:nq], sb_[:, 0:nq], 0.125, tb[:, tsel, a0 * 64:a0 * 64 + 256],
                                                         ALU.mult, ALU.add),
                 reads=[sres, tbres], writes=[sres, tres])
            T.op("act", lambda e: e.activation(pt_[:, 0:nq], tm[:, 0:nq], AF.Exp), reads=[tres], writes=[ptres])
        accO, oRes, _, _ = self.attn_scores_pv(A, KT, A["Kres"][s_], P, QT, A["Qres"][s_][b], P, V, A["Vres"][s_],
                                               q0, nq, ktiles, 0.125, False, bias_fn=bias_fn)
        self.norm_o64(A, accO, oRes, b, q0, nq, half=half)

    def mixer_mla(self, l, with_ctx):
        T = self.T
        A = self.attn_setup(self.rope32_d)
        self.gq_res = Res("gq")
        blocks = [0, 1, 2, 3, 4]
        self.modnorm(l, 1, blocks, lambda b, c: A["hT"][:, c, BLOCKS[b][0]:BLOCKS[b][0] + BLOCKS[b][1]], lambda b: A["hres"][b])
        qblocks = [0, 1, 2, 3] + ([4] if with_ctx else [])
        cqn = self.Rview(53248, [128, 2, NTOK], BF16)
        ckvn = self.Rview(53248 + 9216, [128, NTOK], BF16)
        kpe = self.Rview(53248 + 13824, [128, NTOK], BF16)
        cres = [Res("lat%d" % b) for b in range(5)]
        for r_ in cres:
            alias(r_, [self.Rres])
        gm = self.sb("gm_l%d" % l, [128, 4], F32)
        gmres = Res("gm")
        T.op("dve", lambda e: e.tensor_scalar(gm[:, 0:2], self.vecT[:, V_MQN:V_MQN + 2], 16.0, None, ALU.mult),
             reads=[self.vec_res], writes=[gmres])
        T.op("dve", lambda e: e.tensor_scalar(gm[:, 2:3], self.vecT[:, V_MKN:V_MKN + 1], float(math.sqrt(128.0)), None, ALU.mult),
             reads=[self.vec_res], writes=[gmres])
        T.op("dve", lambda e: e.memset(gm[:, 3:4], float(256 * EPS)), writes=[gmres])
        wd = self.mla_w_down[0].rearrange("(c p) n -> p c n", p=128)
        vd = lambda t: t[:, 0:3328].rearrange("p (c n) -> p c n", n=416)
        t, wres = self.wslab([(vd, wd)])
        td = vd(t)

        def down_block(b):
            t0, nt, isctx = BLOCKS[b]
            cb = [self.proj_fm(td, wres, c * 128, 128, A, b) for c in range(2)]
            sb_, sbres, rs, rres = self.rms_rstd(lambda c: cb[c][0][:, 0:nt], 2, nt, [cb[0][1], cb[1][1]])
            self.act_rsqrt(rs[:, 0:nt], sb_[:, 0:nt], gm[:, 3:4], [sbres, gmres], [sbres, rres])
            for c in range(2):
                T.op("dve", lambda e, c=c: e.scalar_tensor_tensor(cqn[:, c, t0:t0 + nt], cb[c][0][:, 0:nt], gm[:, c:c + 1], rs[:, 0:nt],
                                                                  ALU.mult, ALU.mult),
                     reads=[cb[c][1], rres, gmres], writes=[cb[c][1], cres[b]])
            kb, kbres = self.proj_fm(td, wres, 256, 128, A, b)
            sb2, sb2res, rs2, rres2 = self.rms_rstd(lambda c: kb[:, 0:nt], 1, nt, [kbres])
            self.act_rsqrt(rs2[:, 0:nt], sb2[:, 0:nt], self.eps128[:, 0:1], [sb2res, self.eps_res], [sb2res, rres2])
            T.op("dve", lambda e: e.scalar_tensor_tensor(ckvn[:, t0:t0 + nt], kb[:, 0:nt], gm[:, 2:3], rs2[:, 0:nt], ALU.mult, ALU.mult),
                 reads=[kbres, rres2, gmres], writes=[kbres, cres[b]])
            pb, pbres = self.proj_fm(td, wres, 320, 96, A, b)
            if not isctx:
                self.rope_to(pb, pbres, A, b, kpe, cres[b], p0=64, np_=32, perm=self.perm32)
            else:
                T.op("act", lambda e: e.copy(kpe[64:96, t0:t0 + nt], pb[64:96, 0:nt]), reads=[pbres], writes=[pbres, cres[b]])
        for b in blocks:
            down_block(b)
        QTs, KTs, Vs, Qr, Kr, Vr = [], [], [], [], [], []
        for s_ in range(2):
            o = s_ * 13824
            QTs.append(self.Rview(o, [128, NTOK], BF16))
            KTs.append(self.Rview(o + 4608, [128, NTOK], BF16))
            Vs.append(self.Rview(o + 9216, [128, 18, 128], BF16))
            Qr.append([Res("mQ%d_%d" % (s_, b)) for b in range(5)])
            Kr.append([Res("mK%d_%d" % (s_, b)) for b in range(5)])
            Vr.append([Res("mV%d_%d" % (s_, b)) for b in range(5)])
            for r_ in Qr[s_] + Kr[s_] + Vr[s_]:
                alias(r_, A["hres"])
        for s_ in range(2):
            T.op("dve", lambda e, s_=s_: e.memset(Vs[s_][:, :, 64:128], 1.0), writes=Vr[s_])
        scale = 96.0 ** -0.5

        W = {}

        def g_q(tu, wres, b, QT, qres):
            t0, nt, isctx = BLOCKS[b]
            qb_, qbres = self.proj_fm(tu, wres, 0, 96, A, b, K_chunks=2,
                                      rhs_fn=lambda k: cqn[:, k, t0:t0 + nt], rhs_res=cres[b])
            if not isctx:
                T.op("act", lambda e: e.copy(QT[0:64, t0:t0 + nt], qb_[0:64, 0:nt]), reads=[qbres], writes=[qbres, qres])
                st2 = self.rope_to(qb_, qbres, A, b, QT, qres, p0=64, np_=32, perm=self.perm32, lag=True)
                yield
                st2()
            else:
                T.op("act", lambda e: e.copy(QT[0:96, t0:t0 + nt], qb_[0:96, 0:nt]), reads=[qbres], writes=[qbres, qres])
            yield

        def f_k(tkv, wres, b, KT, kres):
            t0, nt, isctx = BLOCKS[b]
            kb_, kbres = self.proj_fm(tkv, wres, 0, 64, A, b, K_chunks=1,
                                      rhs_fn=lambda k: ckvn[:, t0:t0 + nt], rhs_res=cres[b])
            T.op("dve", lambda e: e.tensor_copy(KT[0:64, t0:t0 + nt], kb_[0:64, 0:nt]), reads=[kbres], writes=[kbres, kres])
            T.op("dve", lambda e: e.tensor_copy(KT[64:96, t0:t0 + nt], kpe[64:96, t0:t0 + nt]), reads=[cres[b]], writes=[kres])

        def f_v(tkv, wres, b, V, vres):
            t0 = BLOCKS[b][0]
            self.proj_v(tkv, wres, 64, 64, A, b, V, vres, K_chunks=1,
                        lhs_fn=lambda k, j: ckvn[:, t0 + j * 128:t0 + (j + 1) * 128], lhs_res=cres[b])

        def prep(h):
            s_ = h % 2
            wuq = self.mla_w_uq[0, :, h * 96:(h + 1) * 96].rearrange("(c p) n -> p c n", p=128)
            wukv = self.mla_w_ukv[0, :, h * 128:(h + 1) * 128]
            vu = lambda t: t[:, 0:192].rearrange("p (c n) -> p c n", n=96)
            vkv = lambda t: t[:, 192:320].rearrange("p (c n) -> p c n", c=1)
            dl = [(vu, wuq), (lambda t: t[:, 192:320], wukv)]
            if h % 2 == 1:
                dl.append((lambda t: t[:, 320:1344], self.mla_w_o[0, (h - 1) * 64:(h + 1) * 64, :]))
            t, wres = self.wslab(dl)
            tu, tkv, two = vu(t), vkv(t), t[:, 320:1344]
            W[h] = (two, wres)
            QT, KT, V = QTs[s_], KTs[s_], Vs[s_]
            items = []
            for b in blocks:
                if b in qblocks:
                    items.append((g_q(tu, wres, b, QT, Qr[s_][b]), 2))
                items.append((self.g_call(lambda b=b: f_k(tkv, wres, b, KT, Kr[s_][b])), 1))
                items.append((self.g_call(lambda b=b: f_v(tkv, wres, b, V, Vr[s_][b])), 1))
            return items

        def attn(h):
            s_ = h % 2
            two, wres = W[h]
            QT, KT, V = QTs[s_], KTs[s_], Vs[s_]
            for b in qblocks:
                q0, nq, isctx = BLOCKS[b]
                ktiles = list(range(18)) if not isctx else [16, 17]
                P = slice(0, 96)
                accO, oRes, _, _ = self.attn_scores_pv(A, KT, Kr[s_], P, QT, Qr[s_][b], P, V, Vr[s_], q0, nq, ktiles, scale, False)
                self.norm_o64(A, accO, oRes, b, q0, nq, half=h % 2)
                if h % 2 == 1:
                    self.enqueue(self.g_wo(two, wres, A, l, b, q0, nq, rows=128), 4, "wo")
        ntile = 4 * 18 + (2 if with_ctx else 0)
        self.run_units(16, prep, attn, lambda u: ntile)
        for s_ in range(2):
            for r_ in Qr[s_] + Kr[s_] + Vr[s_]:
                for hr in A["hres"]:
                    alias(hr, [r_])
        for r_ in cres:
            alias(self.Rres, [r_])
        self.attn_done(A)

    def mixer_da(self, l, with_ctx):
        T = self.T
        lam_init = 0.8 - 0.6 * math.exp(-0.3 * l)
        A = self.attn_setup(self.rope64_d)
        sg0, sg0res = self.sg.items[0]
        lamt = sg0[:, 0:256]
        lamp = sg0[:, 256:384]
        lams = self.sb("lams", [128, 8], F32)
        lres = sg0res
        lres2 = Res("lam")
        sem = self.new_dma_sem()
        T.dma("sp", lambda e: e.dma_start(out=lamt[:], in_=self.lam_d[0:1, :].broadcast_to([128, 256])), sem, writes=[lres])
        T.op("dve", lambda e: e.tensor_tensor(lamp[:, 0:64], lamt[:, 0:64], lamt[:, 64:128], ALU.mult), reads=[lres], writes=[lres])
        T.op("dve", lambda e: e.tensor_tensor(lamp[:, 64:128], lamt[:, 128:192], lamt[:, 192:256], ALU.mult), reads=[lres], writes=[lres])
        T.op("dve", lambda e: e.reduce_sum(lams[:, 0:1], lamp[:, 0:64], mybir.AxisListType.X), reads=[lres], writes=[lres])
        T.op("dve", lambda e: e.reduce_sum(lams[:, 1:2], lamp[:, 64:128], mybir.AxisListType.X), reads=[lres], writes=[lres])
        T.op("act", lambda e: e.activation(lams[:, 2:4], lams[:, 0:2], AF.Exp), reads=[lres], writes=[lres])
        T.op("dve", lambda e: e.tensor_tensor(lams[:, 4:5], lams[:, 3:4], lams[:, 2:3], ALU.subtract), reads=[lres], writes=[lres])
        T.op("dve", lambda e: e.tensor_scalar(lams[:, 5:6], lams[:, 4:5], float(-lam_init), None, ALU.add), reads=[lres], writes=[lres])
        neglam = lams[:, 5:6]
        T.op("dve", lambda e: e.tensor_scalar(lams[:, 6:7], self.vecT[:, V_SUB:V_SUB + 1],
                                              float((1.0 - lam_init) * math.sqrt(128.0)), None, ALU.mult),
             reads=[self.vec_res], writes=[lres])
        T.op("dve", lambda e: e.memset(lams[:, 7:8], float(128 * EPS)), writes=[lres])
        sgv = lams[:, 6:7]
        eps128 = lams[:, 7:8]

        blocks = [0, 1, 2, 3, 4]
        self.modnorm(l, 1, blocks, lambda b, c: A["hT"][:, c, BLOCKS[b][0]:BLOCKS[b][0] + BLOCKS[b][1]], lambda b: A["hres"][b])
        qblocks = [0, 1, 2, 3] + ([4] if with_ctx else [])
        W = {}

        def g_qk(tw, wres, b, dst, dres):
            t0, nt, isctx = BLOCKS[b]
            bank, bres = self.proj_fm(tw, wres, 0, 128, A, b)
            if not isctx:
                st2 = self.rope_to(bank, bres, A, b, dst, dres, lag=True)
                yield
                st2()
            else:
                T.op("act", lambda e: e.copy(dst[:, t0:t0 + nt], bank[:, 0:nt]), reads=[bres], writes=[bres, dres])
            yield

        def prep(h):
            s_ = h % 2
            wq = self.da_w_qkv[0, :, h * 128:(h + 1) * 128].rearrange("(c p) n -> p c n", p=128)
            wk = self.da_w_qkv[0, :, D + h * 128:D + (h + 1) * 128].rearrange("(c p) n -> p c n", p=128)
            wvv = self.da_w_qkv[0, :, 2 * D + h * 128:2 * D + (h + 1) * 128].rearrange("(c p) n -> p c n", p=128)
            wo_src = self.da_w_o[0, h * 128:(h + 1) * 128, :]
            v3 = lambda t, i: t[:, i * 1024:(i + 1) * 1024].rearrange("p (c n) -> p c n", n=128)
            t, wres = self.wslab([(lambda t: v3(t, 0), wq), (lambda t: v3(t, 1), wk), (lambda t: v3(t, 2), wvv),
                                  (lambda t: t[:, 3072:4096], wo_src)])
            tq, tk, tv, two = v3(t, 0), v3(t, 1), v3(t, 2), t[:, 3072:4096]
            W[h] = (two, wres)
            QT, KT, V = A["QT"][s_], A["KT"][s_], A["V"][s_]
            items = []
            for b in blocks:
                if b in qblocks:
                    items.append((g_qk(tq, wres, b, QT, A["Qres"][s_][b]), 2))
                items.append((g_qk(tk, wres, b, KT, A["Kres"][s_][b]), 2))
                items.append((self.g_call(lambda b=b: self.proj_v(tv, wres, 0, 128, A, b, V, A["Vres"][s_][b])), 1))
            return items

        def qblk(h, b):
            s_ = h % 2
            two, wres = W[h]
            QT, KT, V = A["QT"][s_], A["KT"][s_], A["V"][s_]
            q0, nq, isctx = BLOCKS[b]
            ktiles = list(range(18)) if not isctx else [16, 17]
            AB = []
            for m in range(2):
                P = slice(64 * m, 64 * m + 64)
                accO, oRes, accD, dRes = self.attn_scores_pv(A, KT, A["Kres"][s_], P, QT, A["Qres"][s_][b], P,
                                                             V, A["Vres"][s_], q0, nq, ktiles, 0.125, True)
                rd, rdres = self.rstd.next()
                self.act_recip(rd[:, 0:nq], accD[:, 0:nq], [dRes], [dRes, rdres])
                ab, abres = self.tmp.next()
                T.op("dve", lambda e, ab=ab, accO=accO, rd=rd: e.tensor_tensor(ab[:, 0:nq], accO[:, 0:nq], rd[:, 0:nq], ALU.mult),
                     reads=[oRes, rdres], writes=[oRes, abres])
                if m == 0:
                    pin(abres)
                AB.append((ab, abres))
            unpin(AB[0][1])
            o, ores = self.sg.next()
            T.op("dve", lambda e: e.scalar_tensor_tensor(o[:, 0:nq], AB[1][0][:, 0:nq], neglam, AB[0][0][:, 0:nq], ALU.mult, ALU.add),
                 reads=[AB[0][1], AB[1][1], lres], writes=[ores])
            bank, bres, rs, rres = self.rms_rstd(lambda c: o[:, 0:nq], 1, nq, [ores])
            self.act_rsqrt(rs[:, 0:nq], bank[:, 0:nq], eps128, [bres, lres], [bres, rres])
            T.op("dve", lambda e: e.scalar_tensor_tensor(A["Oh"][:, q0:q0 + nq], o[:, 0:nq], sgv, rs[:, 0:nq], ALU.mult, ALU.mult),
                 reads=[ores, rres, lres], writes=[A["Ores"][b]])
            self.enqueue(self.g_wo(two, wres, A, l, b, q0, nq, rows=128), 4, "wo")

        def attn(h):
            for b in qblocks:
                qblk(h, b)
        ntile = 2 * (4 * 18 + (2 if with_ctx else 0))
        self.run_units(8, prep, attn, lambda u: ntile)
        self.attn_done(A)

    def final(self, out_d):
        T = self.T
        fg = self.vecT[:, V_FG:V_FG + 8]
        fgs = self.sb("fgs", [128, 8], F32)
        fres = Res("fgs")
        T.op("dve", lambda e: e.tensor_scalar(fgs[:], fg, float(math.sqrt(D)), None, ALU.mult), reads=[self.vec_res], writes=[fres])
        base = 0
        parts = [(self.Rview(base + i * 6144, [128, 3, 1024], BF16), Res("yparts%d" % i)) for i in range(2)]
        r1s = [(self.Rview(base + 12288 + i * 4096, [128, NCH, 128], F32), Res("yr1%d" % i)) for i in range(2)]
        stg = [(self.Rview(base + 12288 + 8192 + i * 4096, [128, 1024], F32), Res("ystg%d" % i), self.new_dma_sem()) for i in range(2)]
        for r in [s[1] for s in stg] + [p[1] for p in parts] + [r[1] for r in r1s]:
            alias(r, [self.Rres])
        for b in range(4):
            t0, nt, _ = BLOCKS[b]
            xr = self.xres[b]
            bank, bres, rs, rres = self.rms_rstd(lambda c, t0=t0, nt=nt: self.xT[:, c, t0:t0 + nt], NCH, nt, [xr])
            self.act_rsqrt(rs[:, 0:nt], bank[:, 0:nt], self.epsD[:, 0:1], [bres, self.eps_res], [bres, rres])
            for c in range(NCH):
                T.op("dve", lambda e, c=c, rs=rs, t0=t0, nt=nt: e.scalar_tensor_tensor(
                    self.xT[:, c, t0:t0 + nt], self.xT[:, c, t0:t0 + nt], fgs[:, c:c + 1], rs[:, 0:nt], ALU.mult, ALU.mult),
                    reads=[xr, rres, fres], writes=[xr])
            for jj in range(4):
                j = b * 4 + jj
                pp, pres = parts[j % 2]
                r1, rres1 = r1s[j % 2]
                st, sres, ssem = stg[j % 2]
                src = self.xT[:, :, j * 128:(j + 1) * 128]
                ppv = pp.rearrange("p t (c n) -> p t c n", n=128)
                T.op("act", lambda e, ppv=ppv, src=src: e.copy(ppv[:, 0], src), reads=[xr], writes=[pres])
                T.op("dve", lambda e, ppv=ppv, src=src, r1=r1: e.tensor_tensor(r1, src, ppv[:, 0], ALU.subtract),
                     reads=[xr, pres], writes=[rres1])
                T.op("act", lambda e, ppv=ppv, r1=r1: e.copy(ppv[:, 1], r1), reads=[rres1], writes=[pres])
                T.op("dve", lambda e, ppv=ppv, r1=r1: e.tensor_tensor(r1, r1, ppv[:, 1], ALU.subtract),
                     reads=[rres1, pres], writes=[rres1])
                T.op("act", lambda e, ppv=ppv, r1=r1: e.copy(ppv[:, 2], r1), reads=[rres1], writes=[pres])
                for part in range(3):
                    bank2, bres2 = self.ps_all.next()
                    pb = bank2[:].bitcast(BF16)

                    def mm(e, part=part, pb=pb, pp=pp):
                        ins = None
                        for c in range(NCH):
                            ins = e.transpose(pb[:, c * 128:(c + 1) * 128], pp[:, part, c * 128:(c + 1) * 128], self.ident)
                        return ins
                    T.op("pe", mm, reads=[pres, self.cst_res], writes=[bres2])
                    if part == 0:
                        T.op("act", lambda e, pb=pb, st=st: e.copy(st, pb), reads=[bres2], writes=[bres2, sres])
                    else:
                        T.op("dve", lambda e, pb=pb, st=st: e.tensor_tensor(st, pb, st, ALU.add),
                             reads=[bres2, sres], writes=[bres2, sres])
                T.dma("sp", lambda e, st=st, j=j: e.dma_start(out=out_d[j * 128:(j + 1) * 128, :], in_=st), ssem, reads=[sres])
        self.out_events = [(s[2], T.cnt[s[2]]) for s in stg]

    def emit(self):
        nc, T = self.nc, self.T
        for k, c in self.out_events:
            T.prog["sp"].append(("wait", k, c))
        keys = list(T.prog.keys()) + self.dma_sems
        with contextlib.ExitStack() as es:
            sems = {k: es.enter_context(nc.semaphore("s_" + k)) for k in keys}
            block = es.enter_context(nc.Block())

            def run(engname):
                def body(e):
                    for it in T.prog[engname]:
                        if it[0] == "wait":
                            e.wait_ge(sems[it[1]], it[2])
                        else:
                            _, fn, semkey, inc = it
                            ins = fn(e)
                            ins.then_inc(sems[semkey], inc)
                return body
            block.tensor(run("pe"))
            block.scalar(run("act"))
            block.vector(run("dve"))
            block.gpsimd(run("pool"))
            block.sync(run("sp"))


def host_consts():
    c = np.zeros((5, 128, 128), np.float32)
    for m in range(64, 96):
        dd = (m - 64) % 16
        k = m + 8 if dd < 8 else m - 8
        c[4, k, m] = 1.0
    c[0] = np.eye(128)
    c[1] = 1.0
    c[2, 0:64, 0:64] = 1.0
    c[2, 64:128, 64:128] = 1.0
    for m in range(128):
        k = m + 16 if (m % 32) < 16 else m - 16
        c[3, k, m] = 1.0
    return c


def host_vecs(inputs, b):
    v = np.zeros((NVEC, 128), np.float32)
    v[V_C:V_C + 8] = inputs["c"][b].reshape(8, 128)
    v[V_CC:V_CC + 8] = inputs["c_ctx"].reshape(8, 128)
    v[V_FG:V_FG + 8] = inputs["final_g"].reshape(8, 128)
    v[V_NG:V_NG + 96] = inputs["norm_g"].reshape(96, 128)
    v[V_BM:V_BM + 288] = inputs["b_mod"].reshape(288, 128)
    v[V_QN] = np.tile(inputs["gqa_q_norm_g"][0], 2)
    v[V_KN] = np.tile(inputs["gqa_k_norm_g"][0], 2)
    v[V_SUB] = inputs["da_subln_g"][0]
    v[V_MQN:V_MQN + 2] = inputs["mla_q_norm_g"][0].reshape(2, 128)
    v[V_MKN] = inputs["mla_kv_norm_g"][0]
    return v


def host_rope(rd):
    half = rd // 2
    nf = half // 2
    inv = (np.float32(10000.0) ** (-np.arange(nf, dtype=np.float32) / np.float32(nf))).astype(np.float32)
    t = np.arange(SEQ)
    row = (t // GRID_W).astype(np.float32)
    col = (t % GRID_W).astype(np.float32)
    out = np.zeros((2, 128, SEQ), np.float32)
    for p in range(128):
        d = p % rd
        pos = row if d < half else col
        dd = d % half
        f = dd % nf
        ang = (pos * inv[f]).astype(np.float32)
        out[0, p] = np.cos(ang)
        sn = np.sin(ang)
        out[1, p] = -sn if dd < nf else sn
    return out


def host_rope32():
    nf = 8
    inv = (np.float32(10000.0) ** (-np.arange(nf, dtype=np.float32) / np.float32(nf))).astype(np.float32)
    t = np.arange(SEQ)
    row = (t // GRID_W).astype(np.float32)
    col = (t % GRID_W).astype(np.float32)
    out = np.zeros((2, 128, SEQ), np.float32)
    for p in range(64, 96):
        d = p - 64
        pos = row if d < 16 else col
        dd = d % 16
        f = dd % nf
        ang = (pos * inv[f]).astype(np.float32)
        out[0, p] = np.cos(ang)
        sn = np.sin(ang)
        out[1, p] = -sn if dd < nf else sn
    return out


def host_na_tables(rpb):
    kl = (np.arange(128) // 64)[:, None, None]
    kc = (np.arange(128) % 64)[:, None, None]
    ap = np.arange(15)[None, :, None]
    qc = np.arange(64)[None, None, :]
    a = ap - kl
    dr = 7 - a
    cs = np.clip(qc - 8, 0, 48)
    colv = (kc >= cs) & (kc < cs + 16)
    ridx = np.clip(dr + 7, 0, 14)
    cidx = np.clip(kc - qc + 15, 0, 30)
    ridx, cidx = np.broadcast_arrays(ridx, cidx)
    out = np.empty((16, 2, 128, 15, 64), np.float32)
    neg = np.float32(-100.0)
    for t in range(2):
        rv = (np.abs(dr) <= 7) if t == 0 else ((dr >= -4) & (dr <= 3))
        mask = np.broadcast_to(rv & colv, (128, 15, 64))
        for h in range(16):
            g = rpb[h][ridx, cidx]
            out[h, t] = np.where(mask, g, neg)
    return out.reshape(16, 2, 128, 960)


_CACHE = {}


def kernel(_cfg=None, **inputs):
    cfg = _cfg or {}
    inputs = {k: np.asarray(v) for k, v in inputs.items()}
    key = repr(sorted(cfg.items()))
    if key not in _CACHE:
        _CACHE[key] = Builder(cfg).build()
    nc = _CACHE[key]
    cst = host_consts()
    rope64 = host_rope(64)
    rope32 = host_rope32()
    na_tb = host_na_tables(inputs["na_rpb"][0])
    in_maps = []
    for b in range(8):
        m = {
            "x": np.ascontiguousarray(inputs["x"][b]),
            "ctx": np.ascontiguousarray(inputs["ctx"][b]),
            "vecs": host_vecs(inputs, b),
            "cst": cst,
            "w_mod": inputs["w_mod"],
        }
        if cfg.get("pre_ffn", True) or cfg.get("post_ffn", True):
            m["w_ffn_in"] = inputs["w_ffn_in"]
            m["w_ffn_out"] = inputs["w_ffn_out"]
        if cfg.get("mixer", True):
            m["rope64"] = rope64
            m["da_w_qkv"] = inputs["da_w_qkv"]
            m["da_w_o"] = inputs["da_w_o"]
            m["rope32"] = rope32
            m["na_tb"] = na_tb
            for k_ in ("gqa_w_qkv", "gqa_w_o", "mla_w_down", "mla_w_uq", "mla_w_ukv", "mla_w_o", "na_w_qkv", "na_w_o"):
                m[k_] = inputs[k_]
            m["lamv"] = np.concatenate([inputs["da_lam_q1"][0], inputs["da_lam_k1"][0], inputs["da_lam_q2"][0],
                                        inputs["da_lam_k2"][0]]).reshape(1, 256).astype(np.float32)
        in_maps.append(m)
    ncores = cfg.get("ncores", 8)
    in_maps = in_maps[:ncores]
    if cfg.get("trace"):
        res = run_bass_kernel_spmd(nc, in_maps, core_ids=list(range(ncores)), trace=True)
        print("EXEC_NS", res.exec_time_ns)
    else:
        res = run_bass_kernel_spmd(nc, in_maps, core_ids=list(range(ncores)))
    out = np.stack([np.asarray(r["out"]) for r in res.results], axis=0).astype(np.float32)
    return out
```

```python
import contextlib
import math
import numpy as np
import concourse.bass as bass
import concourse.mybir as mybir
from concourse.bass_utils import run_bass_kernel_spmd

F32 = mybir.dt.float32
BF16 = mybir.dt.bfloat16
AF = mybir.ActivationFunctionType
ALU = mybir.AluOpType

D = 1024
NCH = 8
SEQ = 2048
CTX = 256
NTOK = SEQ + CTX
DFF = 2816
NFF = 22
DEPTH = 4
EPS = 1e-6
GRID_W = 64
BLOCKS = [(0, 512, False), (512, 512, False), (1024, 512, False), (1536, 512, False), (2048, 256, True)]

V_C, V_CC, V_FG, V_NG, V_BM = 0, 8, 16, 24, 128
V_QN, V_KN, V_SUB, V_MQN, V_MKN = 416, 417, 418, 419, 421
NVEC = 512


class Res:
    __slots__ = ("name", "w", "r")

    def __init__(self, name):
        self.name = name
        self.w = None
        self.r = {}


class Tracker:
    def __init__(self):
        self.prog = {e: [] for e in ("pe", "act", "dve", "pool", "sp")}
        self.cnt = {}
        self.seen = {e: {} for e in self.prog}
        self.nsem_dma = 0

    def _waits(self, en, reads, writes):
        need = {}

        def req(ev):
            if ev is None:
                return
            k, c = ev
            if need.get(k, 0) < c:
                need[k] = c
        for r in reads:
            req(r.w)
        for w in writes:
            req(w.w)
            for k, c in w.r.items():
                req((k, c))
        for k, c in need.items():
            if k == en and en in ("pe",):
                continue
            if self.seen[en].get(k, 0) >= c:
                continue
            self.seen[en][k] = c
            self.prog[en].append(("wait", k, c))

    def op(self, en, fn, reads=(), writes=()):
        self._waits(en, reads, writes)
        c = self.cnt.get(en, 0) + 1
        self.cnt[en] = c
        self.prog[en].append(("op", fn, en, 1))
        for r in reads:
            r.r[en] = c
        for w in writes:
            w.w = (en, c)
            w.r = {}
        return (en, c)

    def dma(self, en, fn, semkey, reads=(), writes=()):
        self._waits(en, reads, writes)
        c = self.cnt.get(semkey, 0) + 16
        self.cnt[semkey] = c
        self.prog[en].append(("op", fn, semkey, 16))
        for r in reads:
            r.r[semkey] = c
        for w in writes:
            w.w = (semkey, c)
            w.r = {}
        return (semkey, c)

    def wait_all(self, en, ress):
        self._waits(en, [], ress)


def alias(new, olds):
    for o in olds:
        if o.w is not None:
            k, c = o.w
            if new.r.get(k, 0) < c:
                new.r[k] = c
        for k, c in o.r.items():
            if new.r.get(k, 0) < c:
                new.r[k] = c


PINNED = set()


class Rot:
    def __init__(self, items):
        self.items = items
        self.i = 0

    def next(self):
        for _ in range(len(self.items)):
            it = self.items[self.i % len(self.items)]
            self.i += 1
            if it[1] not in PINNED:
                return it
        raise RuntimeError("all rotating buffers pinned")


def pin(*ress):
    for r in ress:
        PINNED.add(r)


def unpin(*ress):
    for r in ress:
        PINNED.discard(r)


class Builder:
    def __init__(self, cfg):
        self.cfg = cfg
        self.nc = bass.Bass("TRN2", target_bir_lowering=False)
        self.T = Tracker()
        PINNED.clear()
        self.fillq, self.fill_stages, self.fill_rr, self.tiles_left, self.fill_window = [], 0, 0, 1, 2
        self.dram = {}
        self.dma_sems = []

    def din(self, name, shape, dt=F32):
        t = self.nc.dram_tensor(name, list(shape), dt, kind="ExternalInput").ap()
        self.dram[name] = t
        return t

    def new_dma_sem(self):
        k = "dma%d" % len(self.dma_sems)
        self.dma_sems.append(k)
        return k

    def sb(self, name, shape, dt):
        return self.nc.alloc_sbuf_tensor(name, list(shape), dt)

    def build(self):
        nc, T = self.nc, self.T
        cfg = self.cfg
        x_d = self.din("x", [SEQ, D])
        ctx_d = self.din("ctx", [CTX, D])
        vecs_d = self.din("vecs", [NVEC, 128])
        cst_d = self.din("cst", [5, 128, 128])
        w_mod = self.din("w_mod", [DEPTH, D, 9 * D])
        w_in = w_out = None
        if cfg.get("pre_ffn", True) or cfg.get("post_ffn", True):
            w_in = self.din("w_ffn_in", [DEPTH, 2, D, 2 * DFF])
            w_out = self.din("w_ffn_out", [DEPTH, 2, DFF, D])
        out_d = nc.dram_tensor("out", [SEQ, D], F32, kind="ExternalOutput").ap()
        self.w_in, self.w_out, self.w_mod = w_in, w_out, w_mod
        nl = cfg.get("layers", DEPTH)
        if cfg.get("mixer", True):
            self.rope64_d = self.din("rope64", [2, 128, SEQ])
            self.da_w_qkv = self.din("da_w_qkv", [1, D, 3 * D])
            self.da_w_o = self.din("da_w_o", [1, D, D])
            self.lam_d = self.din("lamv", [1, 256])
            self.gqa_w_qkv = self.din("gqa_w_qkv", [1, D, 1536])
            self.gqa_w_o = self.din("gqa_w_o", [1, D, D])
            self.rope32_d = self.din("rope32", [2, 128, SEQ])
            self.mla_w_down = self.din("mla_w_down", [1, D, 416])
            self.mla_w_uq = self.din("mla_w_uq", [1, 256, 1536])
            self.mla_w_ukv = self.din("mla_w_ukv", [1, 128, 2048])
            self.mla_w_o = self.din("mla_w_o", [1, D, D])
            self.na_w_qkv = self.din("na_w_qkv", [1, D, 3 * D])
            self.na_w_o = self.din("na_w_o", [1, D, D])
            self.na_tb = self.din("na_tb", [16, 2, 128, 960])

        self.xT = self.sb("xT", [128, NCH, NTOK], F32)
        self.xres = [Res("x%d" % b) for b in range(5)]
        RBYTES = 85504
        self.R = self.sb("R", [128, RBYTES // 2], BF16)
        self.Rres = Res("R")
        NSLOT = 3
        self.slots = []
        for i in range(NSLOT):
            t = self.sb("wslot%d" % i, [128, 4096], BF16)
            self.slots.append((t, Res("wslot%d" % i), self.new_dma_sem()))
        self.slot_i = 0
        self.banks = []
        for i in range(8):
            t = nc.alloc_psum_tensor("ps%d" % i, [128, 512], F32)
            self.banks.append((t, Res("ps%d" % i)))
        self.ps_all = Rot(self.banks)
        self.ps_acc = Rot(self.banks[0:4])
        self.ps_s = Rot(self.banks[4:8])
        def rot(name, n, shape, dt):
            return Rot([(self.sb("%s%d" % (name, i), shape, dt), Res("%s%d" % (name, i))) for i in range(n)])
        self.sq = rot("sq", 2, [128, 512], BF16)
        self.rstd = rot("rstd", 2, [128, 512], F32)
        self.modw = self.rstd.items[1][0][:].bitcast(BF16).rearrange("p (c n) -> p c n", n=128)
        self.modw_res = Res("modw")
        self.modw_sem = self.new_dma_sem()
        self.rstd = Rot(self.rstd.items[0:1])
        self.tmp = rot("tmp", 3, [128, 512], F32)
        self.sg = rot("sg", 2, [128, 512], F32)
        self.pt = rot("pt", 4, [128, 512], BF16)
        self.qraw = rot("qraw", 2, [128, 512], BF16)
        self.cst = self.sb("cst_sb", [128, 5, 128], BF16)
        self.cst_res = Res("cst")
        self.vecT = self.sb("vecT", [128, NVEC], F32)
        self.vec_res = Res("vecT")
        self.silu_c = self.sb("silu_c", [128, NCH, 2], BF16)
        self.silu_res = Res("silu_c")
        self.modbufs = [(self.sb("modsb%d" % i, [128, 72, 2], F32), Res("modsb%d" % i)) for i in range(2)]
        self.modsb, self.mod_res = self.modbufs[0]
        self.mod_ready = set()
        self.sideq, self.side_cur = [], None
        self.lv = self.sb("lv", [128, 9, NCH, 2], F32)
        self.lv_res = Res("lv")
        self.epsD = self.sb("epsD", [128, 1], F32)
        self.eps64 = self.sb("eps64", [128, 1], F32)
        self.eps128 = self.sb("eps128", [128, 1], F32)
        self.eps_res = Res("eps")

        self.ident = self.cst[:, 0, :]
        self.ones = self.cst[:, 1, :]
        self.blk64 = self.cst[:, 2, :]
        self.perm64 = self.cst[:, 3, :]
        self.perm32 = self.cst[:, 4, :]
        self.ones_wide = self.cst[:].rearrange("p c n -> p (c n)")

        T.dma("pool", lambda e: e.dma_start(out=self.cst[:], in_=cst_d.rearrange("c p n -> p c n")),
              self.new_dma_sem(), writes=[self.cst_res])
        T.op("dve", lambda e: e.memset(self.epsD[:], float(D * EPS)), writes=[self.eps_res])
        T.op("dve", lambda e: e.memset(self.eps64[:], float(64 * EPS)), writes=[self.eps_res])
        T.op("dve", lambda e: e.memset(self.eps128[:], float(128 * EPS)), writes=[self.eps_res])
        self.load_vecs(vecs_d)
        self.load_x(x_d, ctx_d)
        ll = cfg.get("layer_list", list(range(DEPTH)))
        for i, l in enumerate(ll):
            self.layer(l, ll[i + 1] if i + 1 < len(ll) else None)
        self.final(out_d)
        self.emit()
        return nc

    def Rview(self, off_bytes, shape, dt):
        n = int(np.prod(shape[1:]))
        if dt == BF16:
            ap = self.R[:, off_bytes // 2: off_bytes // 2 + n]
        else:
            ap = self.R[:, off_bytes // 2: off_bytes // 2 + 2 * n].bitcast(F32)
        if len(shape) == 3:
            ap = ap.rearrange("p (a b) -> p a b", b=shape[2])
        return ap

    def split3_T(self, src_ap, src_res, nrow, dst_fn, dst_res, stage_bf, stage_res):
        raise NotImplementedError

    def load_vecs(self, vecs_d):
        T = self.T
        st = self.Rview(0, [128, 4, 128], F32)
        hi = self.Rview(2048, [128, 3, 512], BF16)
        r1 = self.Rview(2048 + 3072, [128, 512], F32)
        T.dma("sp", lambda e: e.dma_start(out=st, in_=vecs_d.rearrange("(g p) n -> p g n", p=128)),
              self.new_dma_sem(), writes=[self.Rres])
        stf = st.rearrange("p g n -> p (g n)")
        T.op("dve", lambda e: e.tensor_copy(hi[:, 0, :], stf), reads=[self.Rres], writes=[self.Rres])
        T.op("dve", lambda e: e.tensor_tensor(r1, stf, hi[:, 0, :], ALU.subtract), reads=[self.Rres], writes=[self.Rres])
        T.op("dve", lambda e: e.tensor_copy(hi[:, 1, :], r1), reads=[self.Rres], writes=[self.Rres])
        T.op("dve", lambda e: e.tensor_tensor(r1, r1, hi[:, 1, :], ALU.subtract), reads=[self.Rres], writes=[self.Rres])
        T.op("dve", lambda e: e.tensor_copy(hi[:, 2, :], r1), reads=[self.Rres], writes=[self.Rres])
        for part in range(3):
            bank, bres = self.ps_all.next()
            pb = bank[:].bitcast(BF16)
            def mm(e, part=part, pb=pb):
                ins = None
                for g in range(4):
                    ins = e.transpose(pb[:, g * 128:(g + 1) * 128], hi[:, part, g * 128:(g + 1) * 128], self.ident)
                return ins
            T.op("pe", mm, reads=[self.Rres, self.cst_res], writes=[bres])
            if part == 0:
                T.op("act", lambda e, pb=pb: e.copy(self.vecT[:], pb[:, 0:512]), reads=[bres], writes=[bres, self.vec_res])
            else:
                T.op("dve", lambda e, pb=pb: e.tensor_tensor(self.vecT[:], pb[:, 0:512], self.vecT[:], ALU.add),
                     reads=[bres, self.vec_res], writes=[bres, self.vec_res])
        T.op("act", lambda e: e.activation(self.silu_c[:, :, 0], self.vecT[:, V_C:V_C + 8], AF.Silu),
             reads=[self.vec_res], writes=[self.silu_res])
        T.op("act", lambda e: e.activation(self.silu_c[:, :, 1], self.vecT[:, V_CC:V_CC + 8], AF.Silu),
             reads=[self.vec_res], writes=[self.silu_res])

    def load_x(self, x_d, ctx_d):
        T = self.T
        base = 8192
        stg = [(self.Rview(base + i * 4096, [128, 1024], F32), Res("xstg%d" % i), self.new_dma_sem()) for i in range(2)]
        parts = [(self.Rview(base + 8192 + i * 6144, [128, 3, 1024], BF16), Res("xparts%d" % i)) for i in range(2)]
        r1s = [(self.Rview(base + 8192 + 12288 + i * 4096, [128, 1024], F32), Res("xr1%d" % i)) for i in range(2)]
        for r in [s[1] for s in stg] + [p[1] for p in parts] + [r[1] for r in r1s]:
            alias(r, [self.Rres])
        def stage1(j):
            st, sres, ssem = stg[j % 2]
            pp, pres = parts[j % 2]
            r1, rres = r1s[j % 2]
            src = x_d[j * 128:(j + 1) * 128, :] if j < 16 else ctx_d[(j - 16) * 128:(j - 15) * 128, :]
            T.dma("sp", lambda e: e.dma_start(out=st, in_=src), ssem, writes=[sres])
            T.op("act", lambda e: e.copy(pp[:, 0, :], st), reads=[sres], writes=[pres])
            T.op("dve", lambda e: e.tensor_tensor(r1, st, pp[:, 0, :], ALU.subtract), reads=[sres, pres], writes=[rres])
            T.op("act", lambda e: e.copy(pp[:, 1, :], r1), reads=[rres], writes=[pres])
            T.op("dve", lambda e: e.tensor_tensor(r1, r1, pp[:, 1, :], ALU.subtract), reads=[rres, pres], writes=[rres])
            T.op("act", lambda e: e.copy(pp[:, 2, :], r1), reads=[rres], writes=[pres])

        def stage2(j):
            pp, pres = parts[j % 2]
            blk = min(j // 4, 4)
            xr = self.xres[blk]
            dst = self.xT[:, :, j * 128:(j + 1) * 128]
            for part in range(3):
                bank, bres = self.ps_all.next()
                pb = bank[:].bitcast(BF16)

                def mm(e, part=part, pb=pb):
                    ins = None
                    for c in range(NCH):
                        ins = e.transpose(pb[:, c * 128:(c + 1) * 128], pp[:, part, c * 128:(c + 1) * 128], self.ident)
                    return ins
                T.op("pe", mm, reads=[pres, self.cst_res], writes=[bres])
                pbv = pb.rearrange("p (c n) -> p c n", n=128)
                if part == 0:
                    T.op("act", lambda e, pbv=pbv: e.copy(dst, pbv), reads=[bres], writes=[bres, xr])
                else:
                    T.op("dve", lambda e, pbv=pbv: e.tensor_tensor(dst, pbv, dst, ALU.add),
                         reads=[bres, xr], writes=[bres, xr])
        nt_ = NTOK // 128
        for j in range(nt_ + 1):
            if j < nt_:
                stage1(j)
            if j >= 1:
                stage2(j - 1)
        for r in [s[1] for s in stg] + [p[1] for p in parts] + [r[1] for r in r1s]:
            alias(self.Rres, [r])

    def wslab(self, dmas):
        T = self.T
        t, res, sem = self.slots[self.slot_i % len(self.slots)]
        self.slot_i += 1
        for i, (dst_fn, src) in enumerate(dmas):
            dst = dst_fn(t)
            T.dma("pool", lambda e, dst=dst, src=src: e.dma_start(out=dst, in_=src), sem,
                  writes=[res] if i == 0 else [])
            if i > 0:
                res.w = (sem, T.cnt[sem])
        return t, res

    def g_mod_slab(self, l, s_i, dst, dres, lag):
        T = self.T
        src = self.w_mod[l, :, s_i * 512:(s_i + 1) * 512].rearrange("(c p) n -> p c n", p=128)
        t, res = self.wslab([(lambda t: t[:].rearrange("p (c n) -> p c n", n=512), src)])
        tv = t[:].rearrange("p (c n) -> p c n", n=512)
        bank, bres = self.ps_s.next()
        pm = bank[:, 0:8].rearrange("p (m r) -> p m r", r=2)

        def mm(e):
            ins = None
            for mi in range(4):
                for k in range(NCH):
                    ins = e.matmul(pm[:, mi, :], lhsT=tv[:, k, mi * 128:(mi + 1) * 128], rhs=self.silu_c[:, k, :],
                                   start=(k == 0), stop=(k == NCH - 1))
            return ins
        T.op("pe", mm, reads=[res, self.silu_res], writes=[bres])
        bm = self.vecT[:, V_BM + 72 * l + 4 * s_i: V_BM + 72 * l + 4 * s_i + 4]
        T.op("dve", lambda e: e.tensor_tensor(dst[:, 4 * s_i:4 * s_i + 4, :], pm, bm.unsqueeze(2).broadcast_to([128, 4, 2]), ALU.add),
             reads=[bres, self.vec_res], writes=[bres, dres])
        yield

    def g_mod_chunk(self, l, m, dst, dres, lag):
        T = self.T
        src = self.w_mod[l, :, m * 128:(m + 1) * 128].rearrange("(c p) n -> p c n", p=128)
        T.dma("pool", lambda e: e.dma_start(out=self.modw, in_=src), self.modw_sem, writes=[self.modw_res])
        for _ in range(lag):
            yield
        bank, bres = self.ps_s.next()

        def mm(e):
            ins = None
            for k in range(NCH):
                ins = e.matmul(bank[:, 0:2], lhsT=self.modw[:, k, :], rhs=self.silu_c[:, k, :],
                               start=(k == 0), stop=(k == NCH - 1))
            return ins
        T.op("pe", mm, reads=[self.modw_res, self.silu_res], writes=[bres])
        bm = self.vecT[:, V_BM + 72 * l + m: V_BM + 72 * l + m + 1]
        T.op("dve", lambda e: e.tensor_tensor(dst[:, m, :], bank[:, 0:2], bm.broadcast_to([128, 2]), ALU.add),
             reads=[bres, self.vec_res], writes=[bres, dres])
        yield

    def mod_items(self, l, lag):
        dst, dres = self.modbufs[l % 2]
        if lag:
            return [self.g_mod_chunk(l, m, dst, dres, lag) for m in range(72)]
        return [self.g_mod_slab(l, s_i, dst, dres, lag) for s_i in range(18)]

    def compute_mod(self, l):
        T = self.T
        if l not in self.mod_ready:
            for g in self.mod_items(l, 0):
                for _ in g:
                    pass
            self.mod_ready.add(l)
        self.modsb, self.mod_res = self.modbufs[l % 2]
        md = self.modsb[:].rearrange("p (j c) r -> p j c r", c=NCH)
        for n in range(3):
            g = self.vecT[:, V_NG + (l * 3 + n) * 8: V_NG + (l * 3 + n + 1) * 8]
            T.op("dve", lambda e, n=n: e.tensor_copy(self.lv[:, 3 * n, :, :], md[:, 3 * n, :, :]),
                 reads=[self.mod_res], writes=[self.lv_res])
            T.op("dve", lambda e, n=n: e.tensor_scalar(self.lv[:, 3 * n + 1, :, :], md[:, 3 * n + 1, :, :],
                                                       1.0, float(math.sqrt(D)), ALU.add, ALU.mult),
                 reads=[self.mod_res], writes=[self.lv_res])
            T.op("dve", lambda e, n=n, g=g: e.tensor_tensor(self.lv[:, 3 * n + 1, :, :], self.lv[:, 3 * n + 1, :, :],
                                                            g.unsqueeze(2).broadcast_to([128, NCH, 2]), ALU.mult),
                 reads=[self.lv_res, self.vec_res], writes=[self.lv_res])
            sc = 1.0 if n == 1 else 0.5
            T.op("dve", lambda e, n=n, sc=sc: e.tensor_scalar(self.lv[:, 3 * n + 2, :, :], md[:, 3 * n + 2, :, :],
                                                              sc, None, ALU.mult),
                 reads=[self.mod_res], writes=[self.lv_res])

    def rms_rstd(self, src_fn, nch, ntok, reads, ones_ap=None, np_=128):
        T = self.T
        bank, bres = self.ps_s.next()
        sqs = []
        for c in range(nch):
            sq, sres = self.sq.next()
            T.op("act", lambda e, sq=sq, c=c: e.activation(sq[0:np_, 0:ntok], src_fn(c), AF.Square), reads=reads, writes=[sres])
            T.op("pe", lambda e, sq=sq, c=c: e.matmul(bank[0:np_, 0:ntok], lhsT=self.ones[0:np_, 0:np_],
                                                     rhs=sq[0:np_, 0:ntok], start=(c == 0), stop=(c == nch - 1)),
                 reads=[sres, self.cst_res], writes=[bres])
        rs, rres = self.rstd.next()
        return bank, bres, rs, rres

    def modnorm(self, l, n, blocks, dst_fn, dst_res_fn):
        T = self.T
        for b in blocks:
            t0, nt, isctx = BLOCKS[b]
            r = 1 if isctx else 0
            xr = self.xres[b]
            bank, bres, rs, rres = self.rms_rstd(lambda c, t0=t0, nt=nt: self.xT[:, c, t0:t0 + nt], NCH, nt, [xr])
            self.act_rsqrt(rs[:, 0:nt], bank[:, 0:nt], self.epsD[:, 0:1], [bres, self.eps_res], [bres, rres])
            dres = dst_res_fn(b)
            for c in range(NCH):
                tm, tres = self.tmp.next()
                gs = self.lv[:, 3 * n + 1, c, r:r + 1]
                sh = self.lv[:, 3 * n, c, r:r + 1]
                T.op("dve", lambda e, tm=tm, c=c, gs=gs, rs=rs, nt=nt, t0=t0: e.scalar_tensor_tensor(
                    tm[:, 0:nt], self.xT[:, c, t0:t0 + nt], gs, rs[:, 0:nt], ALU.mult, ALU.mult),
                    reads=[xr, rres, self.lv_res], writes=[tres])
                T.op("act", lambda e, tm=tm, c=c, sh=sh, nt=nt, b=b: e.activation(dst_fn(b, c), tm[:, 0:nt], AF.Identity, bias=sh),
                     reads=[tres, self.lv_res], writes=[dres])

    def ffn(self, l, j, with_ctx):
        T = self.T
        n = 0 if j == 0 else 2
        HB = 18432
        passes = [[0, 1], [2, 3]]
        for pi, pblocks in enumerate(passes):
            hh = self.Rview(0, [128, NCH, 1152], BF16)
            uu = self.Rview(HB, [128, NFF, 1152], BF16)
            hres = [Res("hh%d" % i) for i in range(3)]
            ures = [Res("uu%d" % i) for i in range(3)]
            for r_ in hres + ures:
                alias(r_, [self.Rres])
            subs = [(BLOCKS[b][0], 512, i * 512, self.xres[b], False) for i, b in enumerate(pblocks)]
            if with_ctx:
                subs.append((2048 + pi * 128, 128, 1024, self.xres[4], True))
            for si, (t0, nt, lo, xr, isctx) in enumerate(subs):
                r = 1 if isctx else 0
                bank, bres, rs, rres = self.rms_rstd(lambda c, t0=t0, nt=nt: self.xT[:, c, t0:t0 + nt], NCH, nt, [xr])
                self.act_rsqrt(rs[:, 0:nt], bank[:, 0:nt], self.epsD[:, 0:1], [bres, self.eps_res], [bres, rres])
                for c in range(NCH):
                    tm, tres = self.tmp.next()
                    gs = self.lv[:, 3 * n + 1, c, r:r + 1]
                    sh = self.lv[:, 3 * n, c, r:r + 1]
                    T.op("dve", lambda e, tm=tm, c=c, gs=gs, rs=rs, nt=nt, t0=t0: e.scalar_tensor_tensor(
                        tm[:, 0:nt], self.xT[:, c, t0:t0 + nt], gs, rs[:, 0:nt], ALU.mult, ALU.mult),
                        reads=[xr, rres, self.lv_res], writes=[tres])
                    T.op("act", lambda e, tm=tm, c=c, sh=sh, nt=nt, lo=lo: e.activation(hh[:, c, lo:lo + nt], tm[:, 0:nt], AF.Identity, bias=sh),
                         reads=[tres, self.lv_res], writes=[hres[si]])
            for s in range(NFF // 2):
                srcg = self.w_in[l, j, :, s * 256:(s + 1) * 256].rearrange("(c p) n -> p c n", p=128)
                srcu = self.w_in[l, j, :, DFF + s * 256:DFF + (s + 1) * 256].rearrange("(c p) n -> p c n", p=128)
                t, res = self.wslab([
                    (lambda t: t[:].rearrange("p (c n) -> p c n", n=512)[:, :, 0:256], srcg),
                    (lambda t: t[:].rearrange("p (c n) -> p c n", n=512)[:, :, 256:512], srcu)])
                tv = t[:].rearrange("p (c n) -> p c n", n=512)
                for i in range(2):
                    f = 2 * s + i
                    for si, (t0, nt, lo, xr, isctx) in enumerate(subs):
                        gb, gres = self.ps_all.next()
                        ub, ures_ = self.ps_all.next()

                        def mm(e, ob, col, lo=lo, nt=nt, tv=tv):
                            ins = None
                            for k in range(NCH):
                                ins = e.matmul(ob[:, 0:nt], lhsT=tv[:, k, col:col + 128], rhs=hh[:, k, lo:lo + nt],
                                               start=(k == 0), stop=(k == NCH - 1))
                            return ins
                        T.op("pe", lambda e, gb=gb, i=i, mm=mm: mm(e, gb, i * 128), reads=[res, hres[si]], writes=[gres])
                        T.op("pe", lambda e, ub=ub, i=i, mm=mm: mm(e, ub, 256 + i * 128), reads=[res, hres[si]], writes=[ures_])
                        sgt, sgres = self.sg.next()
                        T.op("act", lambda e, gb=gb, sgt=sgt, nt=nt: e.activation(sgt[:, 0:nt], gb[:, 0:nt], AF.Silu),
                             reads=[gres], writes=[gres, sgres])
                        T.op("dve", lambda e, ub=ub, sgt=sgt, nt=nt, lo=lo, f=f: e.tensor_tensor(
                            uu[:, f, lo:lo + nt], ub[:, 0:nt], sgt[:, 0:nt], ALU.mult),
                            reads=[ures_, sgres], writes=[ures_, ures[si]])
            for m in range(NCH):
                src = self.w_out[l, j, :, m * 128:(m + 1) * 128].rearrange("(c p) n -> p c n", p=128)
                t, res = self.wslab([
                    (lambda t: t[:, 0:NFF * 128].rearrange("p (c n) -> p c n", n=128)[:, 0:11, :], src[:, 0:11, :]),
                    (lambda t: t[:, 0:NFF * 128].rearrange("p (c n) -> p c n", n=128)[:, 11:22, :], src[:, 11:22, :])])
                tv = t[:, 0:NFF * 128].rearrange("p (c n) -> p c n", n=128)
                for si, (t0, nt, lo, xr, isctx) in enumerate(subs):
                    r = 1 if isctx else 0
                    yb, yres = self.ps_all.next()

                    def mm(e, yb=yb, lo=lo, nt=nt, tv=tv):
                        ins = None
                        for k in range(NFF):
                            ins = e.matmul(yb[:, 0:nt], lhsT=tv[:, k, :], rhs=uu[:, k, lo:lo + nt],
                                           start=(k == 0), stop=(k == NFF - 1))
                        return ins
                    T.op("pe", mm, reads=[res, ures[si]], writes=[yres])
                    gh = self.lv[:, 3 * n + 2, m, r:r + 1]
                    T.op("dve", lambda e, yb=yb, gh=gh, m=m, t0=t0, nt=nt: e.scalar_tensor_tensor(
                        self.xT[:, m, t0:t0 + nt], yb[:, 0:nt], gh, self.xT[:, m, t0:t0 + nt], ALU.mult, ALU.add),
                        reads=[yres, self.lv_res, xr], writes=[yres, xr])
            for r_ in hres + ures:
                alias(self.Rres, [r_])

    def layer(self, l, nxt=None):
        cfg = self.cfg
        with_ctx = l < DEPTH - 1
        self.compute_mod(l)
        if cfg.get("pre_ffn", True):
            self.ffn(l, 0, True)
        if cfg.get("mixer", True):
            if nxt is not None and cfg.get("mod_prefetch", True):
                self.sideq = self.mod_items(nxt, 8)
                self.mod_ready.add(nxt)
            self.mixer(l, with_ctx)
            self.side_finish(all_=True)
        if cfg.get("post_ffn", True):
            self.ffn(l, 1, with_ctx)

    def attn_setup(self, rope_d=None):
        T = self.T
        A = {}
        A["hT"] = self.Rview(0, [128, NCH, NTOK], BF16)
        A["hres"] = [Res("hT%d" % b) for b in range(5)]
        A["ropeC"] = self.Rview(36864, [128, SEQ], F32)
        A["ropeS"] = self.Rview(45056, [128, SEQ], F32)
        A["rope_res"] = Res("rope")
        A["QT"], A["KT"], A["V"] = [], [], []
        A["Qres"], A["Kres"], A["Vres"] = [], [], []
        for s_ in range(2):
            o = 53248 + s_ * 13824
            A["QT"].append(self.Rview(o, [128, NTOK], BF16))
            A["KT"].append(self.Rview(o + 4608, [128, NTOK], BF16))
            A["V"].append(self.Rview(o + 9216, [128, 18, 128], BF16))
            A["Qres"].append([Res("Q%d_%d" % (s_, b)) for b in range(5)])
            A["Kres"].append([Res("K%d_%d" % (s_, b)) for b in range(5)])
            A["Vres"].append([Res("V%d_%d" % (s_, b)) for b in range(5)])
        A["Oh"] = self.Rview(80896, [128, NTOK], BF16)
        A["Ores"] = [Res("Oh%d" % b) for b in range(5)]
        allres = A["hres"] + [A["rope_res"]] + A["Ores"]
        for s_ in range(2):
            allres += A["Qres"][s_] + A["Kres"][s_] + A["Vres"][s_]
        for r_ in allres:
            alias(r_, [self.Rres])
        A["allres"] = allres
        if rope_d is not None:
            sem = self.new_dma_sem()
            T.dma("sp", lambda e: e.dma_start(out=A["ropeC"], in_=rope_d[0]), sem, writes=[A["rope_res"]])
            T.dma("sp", lambda e: e.dma_start(out=A["ropeS"], in_=rope_d[1]), sem, writes=[])
            A["rope_res"].w = (sem, T.cnt[sem])
        return A

    def attn_done(self, A):
        for r_ in A["allres"]:
            alias(self.Rres, [r_])

    def proj_fm(self, wv, wres, col0, ncol, A, b, K_chunks=NCH, rhs_fn=None, rhs_res=None):
        T = self.T
        t0, nt, _ = BLOCKS[b]
        bank, bres = self.ps_s.next()
        if rhs_fn is None:
            rhs_fn = lambda k: A["hT"][:, k, t0:t0 + nt]
            rhs_res = A["hres"][b]

        def mm(e):
            ins = None
            for k in range(K_chunks):
                ins = e.matmul(bank[0:ncol, 0:nt], lhsT=wv[:, k, col0:col0 + ncol], rhs=rhs_fn(k),
                               start=(k == 0), stop=(k == K_chunks - 1))
            return ins
        T.op("pe", mm, reads=[wres, rhs_res], writes=[bres])
        return bank, bres

    def rope_to(self, bank, bres, A, b, dst, dres, p0=0, np_=128, perm=None, lag=False):
        T = self.T
        t0, nt, _ = BLOCKS[b]
        perm = self.perm64 if perm is None else perm
        P = slice(p0, p0 + np_)
        qr, qres = self.qraw.next()
        T.op("act", lambda e: e.copy(qr[P, 0:nt], bank[P, 0:nt]), reads=[bres], writes=[bres, qres])
        t1, t1res = self.tmp.next()
        T.op("dve", lambda e: e.tensor_tensor(t1[P, 0:nt], bank[P, 0:nt], A["ropeC"][P, t0:t0 + nt], ALU.mult),
             reads=[bres, A["rope_res"]], writes=[bres, t1res])

        def stage2():
            b2, b2res = self.ps_s.next()
            T.op("pe", lambda e: e.matmul(b2[P, 0:nt], lhsT=perm[P, P], rhs=qr[P, 0:nt], start=True, stop=True),
                 reads=[qres, self.cst_res], writes=[b2res])
            t2, t2res = self.tmp.next()
            T.op("dve", lambda e: e.tensor_tensor(t2[P, 0:nt], b2[P, 0:nt], A["ropeS"][P, t0:t0 + nt], ALU.mult),
                 reads=[b2res, A["rope_res"]], writes=[b2res, t2res])
            T.op("dve", lambda e: e.tensor_tensor(dst[P, t0:t0 + nt], t1[P, 0:nt], t2[P, 0:nt], ALU.add),
                 reads=[t1res, t2res], writes=[dres])
        if lag:
            pin(qres, t1res)

            def stage2_unpin():
                stage2()
                unpin(qres, t1res)
            return stage2_unpin
        stage2()
        return None

    def proj_v(self, wv, wres, col0, ncol, A, b, dstV, dres, dcol0=0, lhs_fn=None, lhs_res=None, K_chunks=NCH):
        T = self.T
        t0, nt, _ = BLOCKS[b]
        ntile = nt // 128
        bank, bres = self.ps_s.next()
        if lhs_fn is None:
            lhs_fn = lambda k, j: A["hT"][:, k, t0 + j * 128: t0 + (j + 1) * 128]
            lhs_res = A["hres"][b]
        bv = bank[:, 0:ntile * ncol].rearrange("p (j n) -> p j n", n=ncol)

        def mm(e):
            ins = None
            for j in range(ntile):
                for k in range(K_chunks):
                    ins = e.matmul(bv[:, j, :], lhsT=lhs_fn(k, j), rhs=wv[:, k, col0:col0 + ncol],
                                   start=(k == 0), stop=(k == K_chunks - 1))
            return ins
        T.op("pe", mm, reads=[wres, lhs_res], writes=[bres])
        T.op("dve", lambda e: e.tensor_copy(dstV[:, t0 // 128: t0 // 128 + ntile, dcol0:dcol0 + ncol], bv),
             reads=[bres], writes=[bres, dres])

    def attn_scores_pv(self, A, KT, Kres, kp, QT, Qres_b, qp, V, Vres, q0, nq, ktiles, scale, sep_den,
                       k2=None, bias_fn=None):
        T = self.T
        accO, oRes = self.ps_acc.next()
        if sep_den:
            accD, dRes = self.ps_acc.next()
        else:
            accD, dRes = None, None
        M = V.shape[2]
        n = len(ktiles)
        G = 2
        groups = [list(range(i, min(i + G, n))) for i in range(0, n, G)]
        ng = len(groups)
        live = {}

        def stage_s(gi):
            idxs = groups[gi]
            banks = [self.ps_s.next() for _ in idxs]
            kbs = [(ktiles[i] // 4 if ktiles[i] < 16 else 4) for i in idxs]

            def mm(e):
                ins = None
                for (sb_, sres), i in zip(banks, idxs):
                    kt = ktiles[i]
                    ins = e.matmul(sb_[:, 0:nq], lhsT=KT[kp, kt * 128:(kt + 1) * 128], rhs=QT[qp, q0:q0 + nq],
                                   start=True, stop=True)
                return ins
            T.op("pe", mm, reads=[Kres[kb] for kb in set(kbs)] + [Qres_b], writes=[b_[1] for b_ in banks])
            pts = []
            for (sb_, sres), i in zip(banks, idxs):
                kt = ktiles[i]
                pt_, ptres = self.pt.next()
                if bias_fn is None:
                    T.op("act", lambda e, sb_=sb_, pt_=pt_: e.activation(pt_[:, 0:nq], sb_[:, 0:nq], AF.Exp, scale=float(scale)),
                         reads=[sres], writes=[sres, ptres])
                else:
                    bias_fn(sb_, sres, pt_, ptres, kt, i)
                pts.append((pt_, ptres))
            live[gi] = (pts, kbs)

        def stage_pv(gi):
            idxs = groups[gi]
            pts, kbs = live.pop(gi)

            def mm2(e):
                ins = None
                for (pt_, ptres), i in zip(pts, idxs):
                    kt = ktiles[i]
                    ins = e.matmul(accO[0:M, 0:nq], lhsT=V[:, kt, :], rhs=pt_[:, 0:nq], start=(i == 0), stop=(i == n - 1))
                    if sep_den:
                        ins = e.matmul(accD[:, 0:nq], lhsT=self.ones, rhs=pt_[:, 0:nq], start=(i == 0), stop=(i == n - 1))
                dn = self.cfg.get("dummy_n", 0)
                if dn:
                    for _ in range(self.cfg.get("dummy_k", 1)):
                        e.matmul(self.banks[7][0][:, 0:dn], lhsT=self.ones, rhs=self.ones_wide[:, 0:dn], start=True, stop=True)
                return ins
            T.op("pe", mm2, reads=[p_[1] for p_ in pts] + [Vres[kb] for kb in set(kbs)] + [self.cst_res],
                 writes=[oRes] + ([dRes] if sep_den else []))
        for gi in range(ng + 1):
            if gi < ng:
                stage_s(gi)
                for _ in groups[gi]:
                    self.fill_tick()
            if gi - 1 >= 0:
                stage_pv(gi - 1)
        return accO, oRes, accD, dRes

    def wo_accum(self, wo, wres, A, l, b, q0, nq, rows=128):
        T = self.T
        isctx = BLOCKS[b][2]
        r = 1 if isctx else 0
        xr = self.xres[b]
        for m in range(NCH):
            yb, yres = self.ps_s.next()
            T.op("pe", lambda e, yb=yb, m=m: e.matmul(yb[:, 0:nq], lhsT=wo[0:rows, m * 128:(m + 1) * 128],
                                                     rhs=A["Oh"][0:rows, q0:q0 + nq], start=True, stop=True),
                 reads=[wres, A["Ores"][b]], writes=[yres])
            g5 = self.lv[:, 5, m, r:r + 1]
            T.op("dve", lambda e, yb=yb, m=m, g5=g5: e.scalar_tensor_tensor(
                self.xT[:, m, q0:q0 + nq], yb[:, 0:nq], g5, self.xT[:, m, q0:q0 + nq], ALU.mult, ALU.add),
                reads=[yres, self.lv_res, xr], writes=[yres, xr])

    def enqueue(self, gen, est, tag="proj"):
        self.fillq.append([gen, est, tag])
        self.fill_stages += est

    def fill_one(self):
        if not self.fillq:
            return False
        idx = self.fill_rr % min(self.fill_window, len(self.fillq))
        self.fill_rr += 1
        ent = self.fillq[idx]
        try:
            next(ent[0])
            ent[1] = max(1, ent[1] - 1)
            self.fill_stages = max(len(self.fillq), self.fill_stages - 1)
        except StopIteration:
            del self.fillq[idx]
            self.fill_stages = max(len(self.fillq), self.fill_stages - ent[1])
        return True

    def side_tick(self):
        if self.side_cur is None and self.sideq:
            self.side_cur = self.sideq.pop(0)
        if self.side_cur is not None:
            try:
                next(self.side_cur)
            except StopIteration:
                self.side_cur = None

    def side_finish(self, all_=False):
        while self.side_cur is not None or (all_ and self.sideq):
            self.side_tick()

    def fill_tick(self):
        self.side_tick()
        n = -(-self.fill_stages // max(1, self.tiles_left))
        self.tiles_left = max(1, self.tiles_left - 1)
        for _ in range(n):
            if not self.fill_one():
                break

    def drain(self, keep_tag=None):
        keep = [e for e in self.fillq if keep_tag is not None and e[2] == keep_tag]
        run = [e for e in self.fillq if not (keep_tag is not None and e[2] == keep_tag)]
        self.fillq = run
        self.fill_stages = sum(e[1] for e in run)
        while self.fillq:
            self.fill_one()
        self.fillq = keep
        self.fill_stages = sum(e[1] for e in keep)

    def run_units(self, n_units, prep, attn, tiles_of):
        self.fillq, self.fill_stages, self.fill_rr, self.tiles_left = [], 0, 0, 1
        for g, est in prep(0):
            for _ in g:
                pass
        for u in range(n_units):
            if u + 1 < n_units:
                for g, est in prep(u + 1):
                    self.enqueue(g, est, "proj")
            self.tiles_left = tiles_of(u)
            attn(u)
            self.side_finish()
            self.drain(keep_tag="wo")
        self.drain()

    def g_wo(self, wo, wres, A, l, b, q0, nq, rows=128):
        T = self.T
        isctx = BLOCKS[b][2]
        r = 1 if isctx else 0
        xr = self.xres[b]
        for m in range(NCH):
            yb, yres = self.ps_s.next()
            T.op("pe", lambda e, yb=yb, m=m: e.matmul(yb[:, 0:nq], lhsT=wo[0:rows, m * 128:(m + 1) * 128],
                                                     rhs=A["Oh"][0:rows, q0:q0 + nq], start=True, stop=True),
                 reads=[wres, A["Ores"][b]], writes=[yres])
            g5 = self.lv[:, 5, m, r:r + 1]
            T.op("dve", lambda e, yb=yb, m=m, g5=g5: e.scalar_tensor_tensor(
                self.xT[:, m, q0:q0 + nq], yb[:, 0:nq], g5, self.xT[:, m, q0:q0 + nq], ALU.mult, ALU.add),
                reads=[yres, self.lv_res, xr], writes=[yres, xr])
            if m % 2 == 1:
                yield

    def g_call(self, fn):
        fn()
        yield

    def g_head64(self, wv, wres, col0, A, b, dst, dres, gvec, rope, norm, np_=64):
        T = self.T
        t0, nt, isctx = BLOCKS[b]
        P = slice(0, np_)
        bank, bres = self.proj_fm(wv, wres, col0, np_, A, b)
        if norm:
            sq, sres = self.sq.next()
            T.op("act", lambda e: e.activation(sq[P, 0:nt], bank[P, 0:nt], AF.Square), reads=[bres], writes=[sres])
            pin(bres, sres)
            yield
            sb_, sbres = self.ps_s.next()
            T.op("pe", lambda e: e.matmul(sb_[P, 0:nt], lhsT=self.blk64[P, P], rhs=sq[P, 0:nt], start=True, stop=True),
                 reads=[sres, self.cst_res], writes=[sbres])
            rs, rres = self.rstd.next()
            self.act_rsqrt(rs[P, 0:nt], sb_[P, 0:nt], self.eps64[P, 0:1], [sbres, self.eps_res], [sbres, rres])
            unpin(bres, sres)
            if rope and not isctx:
                qn, qnres = self.sg.next()
                T.op("dve", lambda e: e.scalar_tensor_tensor(qn[P, 0:nt], bank[P, 0:nt], gvec, rs[P, 0:nt], ALU.mult, ALU.mult),
                     reads=[bres, rres, self.vec_res, self.gq_res], writes=[bres, qnres])
                st2 = self.rope_to(qn, qnres, A, b, dst, dres, p0=0, np_=np_, lag=True)
                yield
                st2()
            else:
                T.op("dve", lambda e: e.scalar_tensor_tensor(dst[P, t0:t0 + nt], bank[P, 0:nt], gvec, rs[P, 0:nt], ALU.mult, ALU.mult),
                     reads=[bres, rres, self.vec_res, self.gq_res], writes=[bres, dres])
        else:
            if rope and not isctx:
                st2 = self.rope_to(bank, bres, A, b, dst, dres, p0=0, np_=np_, lag=True)
                yield
                st2()
            else:
                T.op("act", lambda e: e.copy(dst[P, t0:t0 + nt], bank[P, 0:nt]), reads=[bres], writes=[bres, dres])
        yield

    def act_rsqrt(self, dst, src, eps_ap, reads, writes):
        T = self.T
        T.op("act", lambda e: e.activation(dst, src, AF.Ln, bias=eps_ap), reads=reads, writes=writes)
        T.op("act", lambda e: e.activation(dst, dst, AF.Exp, scale=-0.5), reads=[writes[-1]], writes=[writes[-1]])

    def act_recip(self, dst, src, reads, writes):
        T = self.T
        T.op("act", lambda e: e.activation(dst, src, AF.Ln), reads=reads, writes=writes)
        T.op("act", lambda e: e.activation(dst, dst, AF.Exp, scale=-1.0), reads=[writes[-1]], writes=[writes[-1]])

    def set_pools(self, nacc):
        self.ps_acc = Rot(self.banks[0:nacc])
        self.ps_s = Rot(self.banks[nacc:(7 if self.cfg.get("dummy_n", 0) else 8)])

    def flush_pending(self):
        for f in self.pending:
            f()
        self.pending = []

    def mixer(self, l, with_ctx):
        kind = l % 4
        self.pending = []
        self.fillq, self.fill_stages, self.fill_rr, self.tiles_left = [], 0, 0, 1
        self.fill_window = 1 if kind == 0 else 2
        self.set_pools(3 if kind == 0 else 2)
        if kind == 0:
            self.mixer_da(l, with_ctx)
        elif kind == 1:
            self.mixer_gqa(l, with_ctx)
        elif kind == 2:
            self.mixer_mla(l, with_ctx)
        else:
            self.mixer_na(l, with_ctx)

    def head64_qk(self, wv, wres, col0, A, b, dst, dres, gvec, rope, norm, src_rhs=None):
        T = self.T
        t0, nt, isctx = BLOCKS[b]
        P = slice(0, 64)
        bank, bres = self.proj_fm(wv, wres, col0, 64, A, b)
        if norm:
            sb_, sbres, rs, rres = self.rms_rstd(lambda c: bank[P, 0:nt], 1, nt, [bres], np_=64)
            self.act_rsqrt(rs[P, 0:nt], sb_[P, 0:nt], self.eps64[P, 0:1], [sbres, self.eps_res], [sbres, rres])
            if rope and not isctx:
                qn, qnres = self.sg.next()
                T.op("dve", lambda e: e.scalar_tensor_tensor(qn[P, 0:nt], bank[P, 0:nt], gvec, rs[P, 0:nt], ALU.mult, ALU.mult),
                     reads=[bres, rres, self.vec_res, self.gq_res], writes=[bres, qnres])
                self.rope_to(qn, qnres, A, b, dst, dres, p0=0, np_=64)
            else:
                T.op("dve", lambda e: e.scalar_tensor_tensor(dst[P, t0:t0 + nt], bank[P, 0:nt], gvec, rs[P, 0:nt], ALU.mult, ALU.mult),
                     reads=[bres, rres, self.vec_res, self.gq_res], writes=[bres, dres])
        else:
            if rope and not isctx:
                self.rope_to(bank, bres, A, b, dst, dres, p0=0, np_=64)
            else:
                T.op("act", lambda e: e.copy(dst[P, t0:t0 + nt], bank[P, 0:nt]), reads=[bres], writes=[bres, dres])

    def norm_o64(self, A, accO, oRes, b, q0, nq, half=0):
        T = self.T
        rd, rdres = self.rstd.next()
        self.act_recip(rd[0:64, 0:nq], accO[64:128, 0:nq], [oRes], [oRes, rdres])
        T.op("dve", lambda e: e.tensor_tensor(A["Oh"][64 * half:64 * half + 64, q0:q0 + nq], accO[0:64, 0:nq], rd[0:64, 0:nq], ALU.mult),
             reads=[oRes, rdres, A["Ores"][b]], writes=[oRes, A["Ores"][b]])

    def set_v_ones(self, A):
        T = self.T
        for s_ in range(2):
            T.op("dve", lambda e, s_=s_: e.memset(A["V"][s_][:, :, 64:128], 1.0), writes=A["Vres"][s_])

    def mixer_gqa(self, l, with_ctx):
        T = self.T
        A = self.attn_setup(self.rope64_d)
        self.set_v_ones(A)
        self.gq = self.sb("gq_l%d" % l, [128, 2], F32)
        self.gq_res = Res("gq")
        T.op("dve", lambda e: e.tensor_scalar(self.gq[:, 0:1], self.vecT[:, V_QN:V_QN + 1], 8.0, None, ALU.mult),
             reads=[self.vec_res], writes=[self.gq_res])
        T.op("dve", lambda e: e.tensor_scalar(self.gq[:, 1:2], self.vecT[:, V_KN:V_KN + 1], 8.0, None, ALU.mult),
             reads=[self.vec_res], writes=[self.gq_res])
        blocks = [0, 1, 2, 3, 4]
        self.modnorm(l, 1, blocks, lambda b, c: A["hT"][:, c, BLOCKS[b][0]:BLOCKS[b][0] + BLOCKS[b][1]], lambda b: A["hres"][b])
        qblocks = [0, 1, 2, 3] + ([4] if with_ctx else [])
        W = {}

        def prep(u):
            kvh, j = u // 2, u % 2
            ks, qs = kvh % 2, u % 2
            items = []
            if j == 0:
                wq = self.gqa_w_qkv[0, :, kvh * 256:(kvh + 1) * 256].rearrange("(c p) n -> p c n", p=128)
                wk = self.gqa_w_qkv[0, :, 1024 + kvh * 64:1024 + (kvh + 1) * 64].rearrange("(c p) n -> p c n", p=128)
                wvv = self.gqa_w_qkv[0, :, 1280 + kvh * 64:1280 + (kvh + 1) * 64].rearrange("(c p) n -> p c n", p=128)
                vq = lambda t: t[:, 0:2048].rearrange("p (c n) -> p c n", n=256)
                vk = lambda t: t[:, 2048:3072].rearrange("p (c n) -> p c n", n=128)
                vv = lambda t: t[:, 3072:3584].rearrange("p (c n) -> p c n", n=64)
                t, wres = self.wslab([(vq, wq), (lambda t: vk(t)[:, :, 0:64], wk), (lambda t: vk(t)[:, :, 64:128], wk), (vv, wvv)])
                wo_src = self.gqa_w_o[0, kvh * 256:(kvh + 1) * 256, :].rearrange("(g p) n -> p g n", p=128)
                vo = lambda t: t[:, 0:2048].rearrange("p (g n) -> p g n", n=1024)
                t2, wres2 = self.wslab([(vo, wo_src)])
                W[kvh] = (vq(t), vk(t), vv(t), wres, vo(t2), wres2)
                tq, tk, tv, wres, two, wres2 = W[kvh]
                for b in blocks:
                    items.append((self.g_head64(tk, wres, 0, A, b, A["KT"][ks], A["Kres"][ks][b], self.gq[:, 1:2], True, True, np_=128), 3))
                    items.append((self.g_call(lambda b=b, tv=tv, wres=wres, ks=ks: self.proj_v(tv, wres, 0, 64, A, b, A["V"][ks], A["Vres"][ks][b])), 1))
            tq, tk, tv, wres, two, wres2 = W[kvh]
            for b in qblocks:
                items.append((self.g_head64(tq, wres, j * 128, A, b, A["QT"][qs], A["Qres"][qs][b], self.gq[:, 0:1], True, True, np_=128), 3))
            return items

        def attn(u):
            kvh, j = u // 2, u % 2
            ks, qs = kvh % 2, u % 2
            tq, tk, tv, wres, two, wres2 = W[kvh]
            KT, V, QT = A["KT"][ks], A["V"][ks], A["QT"][qs]
            for a in range(2):
                P = slice(64 * a, 64 * a + 64)
                for b in qblocks:
                    q0, nq, isctx = BLOCKS[b]
                    ktiles = list(range(18)) if not isctx else [16, 17]
                    accO, oRes, _, _ = self.attn_scores_pv(A, KT, A["Kres"][ks], P, QT, A["Qres"][qs][b], P,
                                                           V, A["Vres"][ks], q0, nq, ktiles, 0.125, False)
                    self.norm_o64(A, accO, oRes, b, q0, nq, half=a)
                    if a == 1:
                        self.enqueue(self.g_wo(two[:, j, :], wres2, A, l, b, q0, nq, rows=128), 4, "wo")
        ntile = 2 * (4 * 18 + (2 if with_ctx else 0))
        self.run_units(8, prep, attn, lambda u: ntile)
        self.attn_done(A)

    def mixer_na(self, l, with_ctx):
        T = self.T
        A = self.attn_setup(None)
        self.set_v_ones(A)
        self.gq_res = Res("gq")
        blocks = [0, 1, 2, 3, 4]
        self.modnorm(l, 1, blocks, lambda b, c: A["hT"][:, c, BLOCKS[b][0]:BLOCKS[b][0] + BLOCKS[b][1]], lambda b: A["hres"][b])
        TB = [self.Rview(36864 + i * 7680, [128, 2, 960], F32) for i in range(2)]
        TBres = [Res("tb%d" % i) for i in range(2)]
        TBsem = [self.new_dma_sem() for i in range(2)]
        for r_ in TBres:
            alias(r_, [A["rope_res"]])
        W = {}

        def prep(h):
            s_ = h % 2
            wq = self.na_w_qkv[0, :, h * 64:(h + 1) * 64].rearrange("(c p) n -> p c n", p=128)
            wk = self.na_w_qkv[0, :, D + h * 64:D + (h + 1) * 64].rearrange("(c p) n -> p c n", p=128)
            wvv = self.na_w_qkv[0, :, 2 * D + h * 64:2 * D + (h + 1) * 64].rearrange("(c p) n -> p c n", p=128)
            v3 = lambda t, i: t[:, i * 512:(i + 1) * 512].rearrange("p (c n) -> p c n", n=64)
            dl = [(lambda t: v3(t, 0), wq), (lambda t: v3(t, 1), wk), (lambda t: v3(t, 2), wvv)]
            if h % 2 == 1:
                dl.append((lambda t: t[:, 2048:3072], self.na_w_o[0, (h - 1) * 64:(h + 1) * 64, :]))
            t, wres = self.wslab(dl)
            tq, tk, tv, two = v3(t, 0), v3(t, 1), v3(t, 2), t[:, 2048:3072]
            tb, tbres, tbsem = TB[s_], TBres[s_], TBsem[s_]
            T.dma("sp", lambda e: e.dma_start(out=tb, in_=self.na_tb[h].rearrange("t p n -> p t n")), tbsem, writes=[tbres])
            W[h] = (two, wres, tb, tbres)
            QT, KT, V = A["QT"][s_], A["KT"][s_], A["V"][s_]
            items = []
            for b in blocks:
                if b < 4:
                    items.append((self.g_head64(tq, wres, 0, A, b, QT, A["Qres"][s_][b], None, False, False), 1))
                items.append((self.g_head64(tk, wres, 0, A, b, KT, A["Kres"][s_][b], None, False, False), 1))
                items.append((self.g_call(lambda b=b: self.proj_v(tv, wres, 0, 64, A, b, V, A["Vres"][s_][b])), 1))
            return items

        def attn(h):
            s_ = h % 2
            two, wres, tb, tbres = W[h]
            for b in range(4):
                for hb in range(2):
                    self.na_halfblock(A, s_, b, hb, tb, tbres, half=h % 2)
                q0, nq, _ = BLOCKS[b]
                if h % 2 == 1:
                    self.enqueue(self.g_wo(two, wres, A, l, b, q0, nq, rows=128), 4, "wo")
        self.run_units(16, prep, attn, lambda u: 60)
        self.attn_done(A)

    def na_halfblock(self, A, s_, b, hb, tb, tbres, half=0):
        T = self.T
        j = 2 * b + hb
        q0, nq = j * 256, 256
        QT, KT, V = A["QT"][s_], A["KT"][s_], A["V"][s_]
        if j == 0:
            kts, full = [0, 1, 2, 3], True
        elif j == 7:
            kts, full = [12, 13, 14, 15], True
        else:
            kts, full = list(range(2 * j - 2, 2 * j + 4)), False
        tsel = 0 if full else 1
        ktiles = kts + [16, 17]
        P = slice(0, 64)

        def bias_fn(sb_, sres, pt_, ptres, kt, i):
            if kt >= 16:
                T.op("act", lambda e: e.activation(pt_[:, 0:nq], sb_[:, 0:nq], AF.Exp, scale=0.125),
                     reads=[sres], writes=[sres, ptres])
                return
            a0 = 4 * j - 2 * kt + 7
            tm, tres = self.tmp.next()
            T.op("dve", lambda e: e.scalar_tensor_tensor(tm[:, 0:nq], sb_[:, 0:nq], 0.125, tb[:, tsel, a0 * 64:a0 * 64 + 256],
                                                         ALU.mult, ALU.add),
                 reads=[sres, tbres], writes=[sres, tres])
            T.op("act", lambda e: e.activation(pt_[:, 0:nq], tm[:, 0:nq], AF.Exp), reads=[tres], writes=[ptres])
        accO, oRes, _, _ = self.attn_scores_pv(A, KT, A["Kres"][s_], P, QT, A["Qres"][s_][b], P, V, A["Vres"][s_],
                                               q0, nq, ktiles, 0.125, False, bias_fn=bias_fn)
        self.norm_o64(A, accO, oRes, b, q0, nq, half=half)

    def mixer_mla(self, l, with_ctx):
        T = self.T
        A = self.attn_setup(self.rope32_d)
        self.gq_res = Res("gq")
        blocks = [0, 1, 2, 3, 4]
        self.modnorm(l, 1, blocks, lambda b, c: A["hT"][:, c, BLOCKS[b][0]:BLOCKS[b][0] + BLOCKS[b][1]], lambda b: A["hres"][b])
        qblocks = [0, 1, 2, 3] + ([4] if with_ctx else [])
        cqn = self.Rview(53248, [128, 2, NTOK], BF16)
        ckvn = self.Rview(53248 + 9216, [128, NTOK], BF16)
        kpe = self.Rview(53248 + 13824, [128, NTOK], BF16)
        cres = [Res("lat%d" % b) for b in range(5)]
        for r_ in cres:
            alias(r_, [self.Rres])
        gm = self.sb("gm_l%d" % l, [128, 4], F32)
        gmres = Res("gm")
        T.op("dve", lambda e: e.tensor_scalar(gm[:, 0:2], self.vecT[:, V_MQN:V_MQN + 2], 16.0, None, ALU.mult),
             reads=[self.vec_res], writes=[gmres])
        T.op("dve", lambda e: e.tensor_scalar(gm[:, 2:3], self.vecT[:, V_MKN:V_MKN + 1], float(math.sqrt(128.0)), None, ALU.mult),
             reads=[self.vec_res], writes=[gmres])
        T.op("dve", lambda e: e.memset(gm[:, 3:4], float(256 * EPS)), writes=[gmres])
        wd = self.mla_w_down[0].rearrange("(c p) n -> p c n", p=128)
        vd = lambda t: t[:, 0:3328].rearrange("p (c n) -> p c n", n=416)
        t, wres = self.wslab([(vd, wd)])
        td = vd(t)

        def down_block(b):
            t0, nt, isctx = BLOCKS[b]
            cb = [self.proj_fm(td, wres, c * 128, 128, A, b) for c in range(2)]
            sb_, sbres, rs, rres = self.rms_rstd(lambda c: cb[c][0][:, 0:nt], 2, nt, [cb[0][1], cb[1][1]])
            self.act_rsqrt(rs[:, 0:nt], sb_[:, 0:nt], gm[:, 3:4], [sbres, gmres], [sbres, rres])
            for c in range(2):
                T.op("dve", lambda e, c=c: e.scalar_tensor_tensor(cqn[:, c, t0:t0 + nt], cb[c][0][:, 0:nt], gm[:, c:c + 1], rs[:, 0:nt],
                                                                  ALU.mult, ALU.mult),
                     reads=[cb[c][1], rres, gmres], writes=[cb[c][1], cres[b]])
            kb, kbres = self.proj_fm(td, wres, 256, 128, A, b)
            sb2, sb2res, rs2, rres2 = self.rms_rstd(lambda c: kb[:, 0:nt], 1, nt, [kbres])
            self.act_rsqrt(rs2[:, 0:nt], sb2[:, 0:nt], self.eps128[:, 0:1], [sb2res, self.eps_res], [sb2res, rres2])
            T.op("dve", lambda e: e.scalar_tensor_tensor(ckvn[:, t0:t0 + nt], kb[:, 0:nt], gm[:, 2:3], rs2[:, 0:nt], ALU.mult, ALU.mult),
                 reads=[kbres, rres2, gmres], writes=[kbres, cres[b]])
            pb, pbres = self.proj_fm(td, wres, 320, 96, A, b)
            if not isctx:
                self.rope_to(pb, pbres, A, b, kpe, cres[b], p0=64, np_=32, perm=self.perm32)
            else:
                T.op("act", lambda e: e.copy(kpe[64:96, t0:t0 + nt], pb[64:96, 0:nt]), reads=[pbres], writes=[pbres, cres[b]])
        for b in blocks:
            down_block(b)
        QTs, KTs, Vs, Qr, Kr, Vr = [], [], [], [], [], []
        for s_ in range(2):
            o = s_ * 13824
            QTs.append(self.Rview(o, [128, NTOK], BF16))
            KTs.append(self.Rview(o + 4608, [128, NTOK], BF16))
            Vs.append(self.Rview(o + 9216, [128, 18, 128], BF16))
            Qr.append([Res("mQ%d_%d" % (s_, b)) for b in range(5)])
            Kr.append([Res("mK%d_%d" % (s_, b)) for b in range(5)])
            Vr.append([Res("mV%d_%d" % (s_, b)) for b in range(5)])
            for r_ in Qr[s_] + Kr[s_] + Vr[s_]:
                alias(r_, A["hres"])
        for s_ in range(2):
            T.op("dve", lambda e, s_=s_: e.memset(Vs[s_][:, :, 64:128], 1.0), writes=Vr[s_])
        scale = 96.0 ** -0.5

        W = {}

        def g_q(tu, wres, b, QT, qres):
            t0, nt, isctx = BLOCKS[b]
            qb_, qbres = self.proj_fm(tu, wres, 0, 96, A, b, K_chunks=2,
                                      rhs_fn=lambda k: cqn[:, k, t0:t0 + nt], rhs_res=cres[b])
            if not isctx:
                T.op("act", lambda e: e.copy(QT[0:64, t0:t0 + nt], qb_[0:64, 0:nt]), reads=[qbres], writes=[qbres, qres])
                st2 = self.rope_to(qb_, qbres, A, b, QT, qres, p0=64, np_=32, perm=self.perm32, lag=True)
                yield
                st2()
            else:
                T.op("act", lambda e: e.copy(QT[0:96, t0:t0 + nt], qb_[0:96, 0:nt]), reads=[qbres], writes=[qbres, qres])
            yield

        def f_k(tkv, wres, b, KT, kres):
            t0, nt, isctx = BLOCKS[b]
            kb_, kbres = self.proj_fm(tkv, wres, 0, 64, A, b, K_chunks=1,
                                      rhs_fn=lambda k: ckvn[:, t0:t0 + nt], rhs_res=cres[b])
            T.op("dve", lambda e: e.tensor_copy(KT[0:64, t0:t0 + nt], kb_[0:64, 0:nt]), reads=[kbres], writes=[kbres, kres])
            T.op("dve", lambda e: e.tensor_copy(KT[64:96, t0:t0 + nt], kpe[64:96, t0:t0 + nt]), reads=[cres[b]], writes=[kres])

        def f_v(tkv, wres, b, V, vres):
            t0 = BLOCKS[b][0]
            self.proj_v(tkv, wres, 64, 64, A, b, V, vres, K_chunks=1,
                        lhs_fn=lambda k, j: ckvn[:, t0 + j * 128:t0 + (j + 1) * 128], lhs_res=cres[b])

        def prep(h):
            s_ = h % 2
            wuq = self.mla_w_uq[0, :, h * 96:(h + 1) * 96].rearrange("(c p) n -> p c n", p=128)
            wukv = self.mla_w_ukv[0, :, h * 128:(h + 1) * 128]
            vu = lambda t: t[:, 0:192].rearrange("p (c n) -> p c n", n=96)
            vkv = lambda t: t[:, 192:320].rearrange("p (c n) -> p c n", c=1)
            dl = [(vu, wuq), (lambda t: t[:, 192:320], wukv)]
            if h % 2 == 1:
                dl.append((lambda t: t[:, 320:1344], self.mla_w_o[0, (h - 1) * 64:(h + 1) * 64, :]))
            t, wres = self.wslab(dl)
            tu, tkv, two = vu(t), vkv(t), t[:, 320:1344]
            W[h] = (two, wres)
            QT, KT, V = QTs[s_], KTs[s_], Vs[s_]
            items = []
            for b in blocks:
                if b in qblocks:
                    items.append((g_q(tu, wres, b, QT, Qr[s_][b]), 2))
                items.append((self.g_call(lambda b=b: f_k(tkv, wres, b, KT, Kr[s_][b])), 1))
                items.append((self.g_call(lambda b=b: f_v(tkv, wres, b, V, Vr[s_][b])), 1))
            return items

        def attn(h):
            s_ = h % 2
            two, wres = W[h]
            QT, KT, V = QTs[s_], KTs[s_], Vs[s_]
            for b in qblocks:
                q0, nq, isctx = BLOCKS[b]
                ktiles = list(range(18)) if not isctx else [16, 17]
                P = slice(0, 96)
                accO, oRes, _, _ = self.attn_scores_pv(A, KT, Kr[s_], P, QT, Qr[s_][b], P, V, Vr[s_], q0, nq, ktiles, scale, False)
                self.norm_o64(A, accO, oRes, b, q0, nq, half=h % 2)
                if h % 2 == 1:
                    self.enqueue(self.g_wo(two, wres, A, l, b, q0, nq, rows=128), 4, "wo")
        ntile = 4 * 18 + (2 if with_ctx else 0)
        self.run_units(16, prep, attn, lambda u: ntile)
        for s_ in range(2):
            for r_ in Qr[s_] + Kr[s_] + Vr[s_]:
                for hr in A["hres"]:
                    alias(hr, [r_])
        for r_ in cres:
            alias(self.Rres, [r_])
        self.attn_done(A)

    def mixer_da(self, l, with_ctx):
        T = self.T
        lam_init = 0.8 - 0.6 * math.exp(-0.3 * l)
        A = self.attn_setup(self.rope64_d)
        sg0, sg0res = self.sg.items[0]
        lamt = sg0[:, 0:256]
        lamp = sg0[:, 256:384]
        lams = self.sb("lams", [128, 8], F32)
        lres = sg0res
        lres2 = Res("lam")
        sem = self.new_dma_sem()
        T.dma("sp", lambda e: e.dma_start(out=lamt[:], in_=self.lam_d[0:1, :].broadcast_to([128, 256])), sem, writes=[lres])
        T.op("dve", lambda e: e.tensor_tensor(lamp[:, 0:64], lamt[:, 0:64], lamt[:, 64:128], ALU.mult), reads=[lres], writes=[lres])
        T.op("dve", lambda e: e.tensor_tensor(lamp[:, 64:128], lamt[:, 128:192], lamt[:, 192:256], ALU.mult), reads=[lres], writes=[lres])
        T.op("dve", lambda e: e.reduce_sum(lams[:, 0:1], lamp[:, 0:64], mybir.AxisListType.X), reads=[lres], writes=[lres])
        T.op("dve", lambda e: e.reduce_sum(lams[:, 1:2], lamp[:, 64:128], mybir.AxisListType.X), reads=[lres], writes=[lres])
        T.op("act", lambda e: e.activation(lams[:, 2:4], lams[:, 0:2], AF.Exp), reads=[lres], writes=[lres])
        T.op("dve", lambda e: e.tensor_tensor(lams[:, 4:5], lams[:, 3:4], lams[:, 2:3], ALU.subtract), reads=[lres], writes=[lres])
        T.op("dve", lambda e: e.tensor_scalar(lams[:, 5:6], lams[:, 4:5], float(-lam_init), None, ALU.add), reads=[lres], writes=[lres])
        neglam = lams[:, 5:6]
        T.op("dve", lambda e: e.tensor_scalar(lams[:, 6:7], self.vecT[:, V_SUB:V_SUB + 1],
                                              float((1.0 - lam_init) * math.sqrt(128.0)), None, ALU.mult),
             reads=[self.vec_res], writes=[lres])
        T.op("dve", lambda e: e.memset(lams[:, 7:8], float(128 * EPS)), writes=[lres])
        sgv = lams[:, 6:7]
        eps128 = lams[:, 7:8]

        blocks = [0, 1, 2, 3, 4]
        self.modnorm(l, 1, blocks, lambda b, c: A["hT"][:, c, BLOCKS[b][0]:BLOCKS[b][0] + BLOCKS[b][1]], lambda b: A["hres"][b])
        qblocks = [0, 1, 2, 3] + ([4] if with_ctx else [])
        W = {}

        def g_qk(tw, wres, b, dst, dres):
            t0, nt, isctx = BLOCKS[b]
            bank, bres = self.proj_fm(tw, wres, 0, 128, A, b)
            if not isctx:
                st2 = self.rope_to(bank, bres, A, b, dst, dres, lag=True)
                yield
                st2()
            else:
                T.op("act", lambda e: e.copy(dst[:, t0:t0 + nt], bank[:, 0:nt]), reads=[bres], writes=[bres, dres])
            yield

        def prep(h):
            s_ = h % 2
            wq = self.da_w_qkv[0, :, h * 128:(h + 1) * 128].rearrange("(c p) n -> p c n", p=128)
            wk = self.da_w_qkv[0, :, D + h * 128:D + (h + 1) * 128].rearrange("(c p) n -> p c n", p=128)
            wvv = self.da_w_qkv[0, :, 2 * D + h * 128:2 * D + (h + 1) * 128].rearrange("(c p) n -> p c n", p=128)
            wo_src = self.da_w_o[0, h * 128:(h + 1) * 128, :]
            v3 = lambda t, i: t[:, i * 1024:(i + 1) * 1024].rearrange("p (c n) -> p c n", n=128)
            t, wres = self.wslab([(lambda t: v3(t, 0), wq), (lambda t: v3(t, 1), wk), (lambda t: v3(t, 2), wvv),
                                  (lambda t: t[:, 3072:4096], wo_src)])
            tq, tk, tv, two = v3(t, 0), v3(t, 1), v3(t, 2), t[:, 3072:4096]
            W[h] = (two, wres)
            QT, KT, V = A["QT"][s_], A["KT"][s_], A["V"][s_]
            items = []
            for b in blocks:
                if b in qblocks:
                    items.append((g_qk(tq, wres, b, QT, A["Qres"][s_][b]), 2))
                items.append((g_qk(tk, wres, b, KT, A["Kres"][s_][b]), 2))
                items.append((self.g_call(lambda b=b: self.proj_v(tv, wres, 0, 128, A, b, V, A["Vres"][s_][b])), 1))
            return items

        def qblk(h, b):
            s_ = h % 2
            two, wres = W[h]
            QT, KT, V = A["QT"][s_], A["KT"][s_], A["V"][s_]
            q0, nq, isctx = BLOCKS[b]
            ktiles = list(range(18)) if not isctx else [16, 17]
            AB = []
            for m in range(2):
                P = slice(64 * m, 64 * m + 64)
                accO, oRes, accD, dRes = self.attn_scores_pv(A, KT, A["Kres"][s_], P, QT, A["Qres"][s_][b], P,
                                                             V, A["Vres"][s_], q0, nq, ktiles, 0.125, True)
                rd, rdres = self.rstd.next()
                self.act_recip(rd[:, 0:nq], accD[:, 0:nq], [dRes], [dRes, rdres])
                ab, abres = self.tmp.next()
                T.op("dve", lambda e, ab=ab, accO=accO, rd=rd: e.tensor_tensor(ab[:, 0:nq], accO[:, 0:nq], rd[:, 0:nq], ALU.mult),
                     reads=[oRes, rdres], writes=[oRes, abres])
                if m == 0:
                    pin(abres)
                AB.append((ab, abres))
            unpin(AB[0][1])
            o, ores = self.sg.next()
            T.op("dve", lambda e: e.scalar_tensor_tensor(o[:, 0:nq], AB[1][0][:, 0:nq], neglam, AB[0][0][:, 0:nq], ALU.mult, ALU.add),
                 reads=[AB[0][1], AB[1][1], lres], writes=[ores])
            bank, bres, rs, rres = self.rms_rstd(lambda c: o[:, 0:nq], 1, nq, [ores])
            self.act_rsqrt(rs[:, 0:nq], bank[:, 0:nq], eps128, [bres, lres], [bres, rres])
            T.op("dve", lambda e: e.scalar_tensor_tensor(A["Oh"][:, q0:q0 + nq], o[:, 0:nq], sgv, rs[:, 0:nq], ALU.mult, ALU.mult),
                 reads=[ores, rres, lres], writes=[A["Ores"][b]])
            self.enqueue(self.g_wo(two, wres, A, l, b, q0, nq, rows=128), 4, "wo")

        def attn(h):
            for b in qblocks:
                qblk(h, b)
        ntile = 2 * (4 * 18 + (2 if with_ctx else 0))
        self.run_units(8, prep, attn, lambda u: ntile)
        self.attn_done(A)

    def final(self, out_d):
        T = self.T
        fg = self.vecT[:, V_FG:V_FG + 8]
        fgs = self.sb("fgs", [128, 8], F32)
        fres = Res("fgs")
        T.op("dve", lambda e: e.tensor_scalar(fgs[:], fg, float(math.sqrt(D)), None, ALU.mult), reads=[self.vec_res], writes=[fres])
        base = 0
        parts = [(self.Rview(base + i * 6144, [128, 3, 1024], BF16), Res("yparts%d" % i)) for i in range(2)]
        r1s = [(self.Rview(base + 12288 + i * 4096, [128, NCH, 128], F32), Res("yr1%d" % i)) for i in range(2)]
        stg = [(self.Rview(base + 12288 + 8192 + i * 4096, [128, 1024], F32), Res("ystg%d" % i), self.new_dma_sem()) for i in range(2)]
        for r in [s[1] for s in stg] + [p[1] for p in parts] + [r[1] for r in r1s]:
            alias(r, [self.Rres])
        def norm_block(b):
            t0, nt, _ = BLOCKS[b]
            xr = self.xres[b]
            bank, bres, rs, rres = self.rms_rstd(lambda c: self.xT[:, c, t0:t0 + nt], NCH, nt, [xr])
            self.act_rsqrt(rs[:, 0:nt], bank[:, 0:nt], self.epsD[:, 0:1], [bres, self.eps_res], [bres, rres])
            for c in range(NCH):
                T.op("dve", lambda e, c=c: e.scalar_tensor_tensor(
                    self.xT[:, c, t0:t0 + nt], self.xT[:, c, t0:t0 + nt], fgs[:, c:c + 1], rs[:, 0:nt], ALU.mult, ALU.mult),
                    reads=[xr, rres, fres], writes=[xr])

        def stage1(j):
            xr = self.xres[j // 4]
            pp, pres = parts[j % 2]
            r1, rres1 = r1s[j % 2]
            src = self.xT[:, :, j * 128:(j + 1) * 128]
            ppv = pp.rearrange("p t (c n) -> p t c n", n=128)
            T.op("act", lambda e: e.copy(ppv[:, 0], src), reads=[xr], writes=[pres])
            T.op("dve", lambda e: e.tensor_tensor(r1, src, ppv[:, 0], ALU.subtract), reads=[xr, pres], writes=[rres1])
            T.op("act", lambda e: e.copy(ppv[:, 1], r1), reads=[rres1], writes=[pres])
            T.op("dve", lambda e: e.tensor_tensor(r1, r1, ppv[:, 1], ALU.subtract), reads=[rres1, pres], writes=[rres1])
            T.op("act", lambda e: e.copy(ppv[:, 2], r1), reads=[rres1], writes=[pres])

        def stage2(j):
            pp, pres = parts[j % 2]
            st, sres, ssem = stg[j % 2]
            for part in range(3):
                bank2, bres2 = self.ps_all.next()
                pb = bank2[:].bitcast(BF16)

                def mm(e, part=part, pb=pb):
                    ins = None
                    for c in range(NCH):
                        ins = e.transpose(pb[:, c * 128:(c + 1) * 128], pp[:, part, c * 128:(c + 1) * 128], self.ident)
                    return ins
                T.op("pe", mm, reads=[pres, self.cst_res], writes=[bres2])
                if part == 0:
                    T.op("act", lambda e, pb=pb: e.copy(st, pb), reads=[bres2], writes=[bres2, sres])
                else:
                    T.op("dve", lambda e, pb=pb: e.tensor_tensor(st, pb, st, ALU.add),
                         reads=[bres2, sres], writes=[bres2, sres])
            T.dma("sp", lambda e: e.dma_start(out=out_d[j * 128:(j + 1) * 128, :], in_=st), ssem, reads=[sres])
        for j in range(17):
            if j < 16:
                if j % 4 == 0:
                    norm_block(j // 4)
                stage1(j)
            if j >= 1:
                stage2(j - 1)
        self.out_events = [(s[2], T.cnt[s[2]]) for s in stg]

    def emit(self):
        nc, T = self.nc, self.T
        for k, c in self.out_events:
            T.prog["sp"].append(("wait", k, c))
        keys = list(T.prog.keys()) + self.dma_sems
        with contextlib.ExitStack() as es:
            sems = {k: es.enter_context(nc.semaphore("s_" + k)) for k in keys}
            block = es.enter_context(nc.Block())

            def run(engname):
                def body(e):
                    for it in T.prog[engname]:
                        if it[0] == "wait":
                            e.wait_ge(sems[it[1]], it[2])
                        else:
                            _, fn, semkey, inc = it
                            ins = fn(e)
                            ins.then_inc(sems[semkey], inc)
                return body
            block.tensor(run("pe"))
            block.scalar(run("act"))
            block.vector(run("dve"))
            block.gpsimd(run("pool"))
            block.sync(run("sp"))


def host_consts():
    c = np.zeros((5, 128, 128), np.float32)
    for m in range(64, 96):
        dd = (m - 64) % 16
        k = m + 8 if dd < 8 else m - 8
        c[4, k, m] = 1.0
    c[0] = np.eye(128)
    c[1] = 1.0
    c[2, 0:64, 0:64] = 1.0
    c[2, 64:128, 64:128] = 1.0
    for m in range(128):
        k = m + 16 if (m % 32) < 16 else m - 16
        c[3, k, m] = 1.0
    return c


def host_vecs(inputs, b):
    v = np.zeros((NVEC, 128), np.float32)
    v[V_C:V_C + 8] = inputs["c"][b].reshape(8, 128)
    v[V_CC:V_CC + 8] = inputs["c_ctx"].reshape(8, 128)
    v[V_FG:V_FG + 8] = inputs["final_g"].reshape(8, 128)
    v[V_NG:V_NG + 96] = inputs["norm_g"].reshape(96, 128)
    v[V_BM:V_BM + 288] = inputs["b_mod"].reshape(288, 128)
    v[V_QN] = np.tile(inputs["gqa_q_norm_g"][0], 2)
    v[V_KN] = np.tile(inputs["gqa_k_norm_g"][0], 2)
    v[V_SUB] = inputs["da_subln_g"][0]
    v[V_MQN:V_MQN + 2] = inputs["mla_q_norm_g"][0].reshape(2, 128)
    v[V_MKN] = inputs["mla_kv_norm_g"][0]
    return v


def host_rope(rd):
    half = rd // 2
    nf = half // 2
    inv = (np.float32(10000.0) ** (-np.arange(nf, dtype=np.float32) / np.float32(nf))).astype(np.float32)
    t = np.arange(SEQ)
    row = (t // GRID_W).astype(np.float32)
    col = (t % GRID_W).astype(np.float32)
    out = np.zeros((2, 128, SEQ), np.float32)
    for p in range(128):
        d = p % rd
        pos = row if d < half else col
        dd = d % half
        f = dd % nf
        ang = (pos * inv[f]).astype(np.float32)
        out[0, p] = np.cos(ang)
        sn = np.sin(ang)
        out[1, p] = -sn if dd < nf else sn
    return out


def host_rope32():
    nf = 8
    inv = (np.float32(10000.0) ** (-np.arange(nf, dtype=np.float32) / np.float32(nf))).astype(np.float32)
    t = np.arange(SEQ)
    row = (t // GRID_W).astype(np.float32)
    col = (t % GRID_W).astype(np.float32)
    out = np.zeros((2, 128, SEQ), np.float32)
    for p in range(64, 96):
        d = p - 64
        pos = row if d < 16 else col
        dd = d % 16
        f = dd % nf
        ang = (pos * inv[f]).astype(np.float32)
        out[0, p] = np.cos(ang)
        sn = np.sin(ang)
        out[1, p] = -sn if dd < nf else sn
    return out


def host_na_tables(rpb):
    kl = (np.arange(128) // 64)[:, None, None]
    kc = (np.arange(128) % 64)[:, None, None]
    ap = np.arange(15)[None, :, None]
    qc = np.arange(64)[None, None, :]
    a = ap - kl
    dr = 7 - a
    cs = np.clip(qc - 8, 0, 48)
    colv = (kc >= cs) & (kc < cs + 16)
    ridx = np.clip(dr + 7, 0, 14)
    cidx = np.clip(kc - qc + 15, 0, 30)
    ridx, cidx = np.broadcast_arrays(ridx, cidx)
    out = np.empty((16, 2, 128, 15, 64), np.float32)
    neg = np.float32(-100.0)
    for t in range(2):
        rv = (np.abs(dr) <= 7) if t == 0 else ((dr >= -4) & (dr <= 3))
        mask = np.broadcast_to(rv & colv, (128, 15, 64))
        for h in range(16):
            g = rpb[h][ridx, cidx]
            out[h, t] = np.where(mask, g, neg)
    return out.reshape(16, 2, 128, 960)


_CACHE = {}


def kernel(_cfg=None, **inputs):
    cfg = _cfg or {}
    inputs = {k: np.asarray(v) for k, v in inputs.items()}
    key = repr(sorted(cfg.items()))
    if key not in _CACHE:
        _CACHE[key] = Builder(cfg).build()
    nc = _CACHE[key]
    cst = host_consts()
    rope64 = host_rope(64)
    rope32 = host_rope32()
    na_tb = host_na_tables(inputs["na_rpb"][0])
    in_maps = []
    for b in range(8):
        m = {
            "x": np.ascontiguousarray(inputs["x"][b]),
            "ctx": np.ascontiguousarray(inputs["ctx"][b]),
            "vecs": host_vecs(inputs, b),
            "cst": cst,
            "w_mod": inputs["w_mod"],
        }
        if cfg.get("pre_ffn", True) or cfg.get("post_ffn", True):
            m["w_ffn_in"] = inputs["w_ffn_in"]
            m["w_ffn_out"] = inputs["w_ffn_out"]
        if cfg.get("mixer", True):
            m["rope64"] = rope64
            m["da_w_qkv"] = inputs["da_w_qkv"]
            m["da_w_o"] = inputs["da_w_o"]
            m["rope32"] = rope32
            m["na_tb"] = na_tb
            for k_ in ("gqa_w_qkv", "gqa_w_o", "mla_w_down", "mla_w_uq", "mla_w_ukv", "mla_w_o", "na_w_qkv", "na_w_o"):
                m[k_] = inputs[k_]
            m["lamv"] = np.concatenate([inputs["da_lam_q1"][0], inputs["da_lam_k1"][0], inputs["da_lam_q2"][0],
                                        inputs["da_lam_k2"][0]]).reshape(1, 256).astype(np.float32)
        in_maps.append(m)
    ncores = cfg.get("ncores", 8)
    in_maps = in_maps[:ncores]
    if cfg.get("trace"):
        res = run_bass_kernel_spmd(nc, in_maps, core_ids=list(range(ncores)), trace=True)
        print("EXEC_NS", res.exec_time_ns)
    else:
        res = run_bass_kernel_spmd(nc, in_maps, core_ids=list(range(ncores)))
    out = np.stack([np.asarray(r["out"]) for r in res.results], axis=0).astype(np.float32)
    return out
```
